# Optimizing a Trainium2 kernel written in Bass

```python
import math
import jax
import jax.numpy as jnp
from jax import lax
import numpy as np

D_MODEL = 1024
BATCH = 4
SEQ = 8192
DEPTH = 2

HEAD_DIM = 64
BRANCH_WIDTH = D_MODEL // 2
N_BRANCH = 3
CONV_WIDTH = BRANCH_WIDTH
CONV_K = 3
FOX_HEADS = BRANCH_WIDTH // HEAD_DIM
SWA_HEADS = BRANCH_WIDTH // HEAD_DIM
SWA_KV_HEADS = 2
SWA_GROUP = SWA_HEADS // SWA_KV_HEADS
WINDOW = 128
BLOCK = 128
N_BUCKETS = 32
MAX_DISTANCE = WINDOW
MEM_LEN = 256
X_HEADS = 4
X_HEAD_DIM = D_MODEL // X_HEADS
_FF_RAW = -(-8 * D_MODEL // 3)
D_FF = -(-_FF_RAW // 256) * 256
IN_COLS = (3 * CONV_WIDTH + 3 * FOX_HEADS * HEAD_DIM + FOX_HEADS
           + (SWA_HEADS + 2 * SWA_KV_HEADS) * HEAD_DIM + N_BRANCH * D_MODEL)
RMS_EPS = 1e-6
NEG_INF = -1e30

kernel_name = "hybrid_conv_fox_swa_block"


def rms_norm(x, g):
    xf = x.astype(jnp.float32)
    y = xf * lax.rsqrt(jnp.mean(xf * xf, axis=-1, keepdims=True) + RMS_EPS)
    return (y * g.astype(jnp.float32)).astype(x.dtype)


def split_proj(proj):
    sizes = ([CONV_WIDTH] * 3 + [FOX_HEADS * HEAD_DIM] * 3 + [FOX_HEADS]
             + [SWA_HEADS * HEAD_DIM, SWA_KV_HEADS * HEAD_DIM, SWA_KV_HEADS * HEAD_DIM]
             + [N_BRANCH * D_MODEL])
    parts, off = [], 0
    for s in sizes:
        parts.append(proj[..., off:off + s])
        off += s
    return parts


def short_conv_branch(gate_b, gate_c, u, conv_w):
    z = gate_c * u
    y = lax.conv_general_dilated(
        z, conv_w[:, None, :].astype(z.dtype), window_strides=(1,),
        padding=[(CONV_K - 1, 0)], dimension_numbers=('NWC', 'WIO', 'NWC'),
        feature_group_count=CONV_WIDTH)
    return gate_b * y


def fox_branch(q, k, v, f_logit, f_bias):
    b, s = q.shape[0], q.shape[1]
    nb = s // BLOCK
    q = q.reshape(b, s, FOX_HEADS, HEAD_DIM)
    k = k.reshape(b, s, FOX_HEADS, HEAD_DIM)
    v = v.reshape(b, s, FOX_HEADS, HEAD_DIM)
    log_f = jax.nn.log_sigmoid(f_logit.astype(jnp.float32) + f_bias.astype(jnp.float32))
    c = jnp.cumsum(log_f, axis=1)
    c_k = c.transpose(0, 2, 1)
    q_blocks = q.reshape(b, nb, BLOCK, FOX_HEADS, HEAD_DIM).transpose(1, 0, 2, 3, 4)
    c_blocks = c.reshape(b, nb, BLOCK, FOX_HEADS).transpose(1, 0, 2, 3)
    starts = jnp.arange(nb, dtype=jnp.int32) * BLOCK
    k_pos = jnp.arange(s, dtype=jnp.int32)
    scale = HEAD_DIM ** -0.5

    def one_block(args):
        qi, ci, start = args
        logits = jnp.einsum('bqhd,bkhd->bhqk', qi, k).astype(jnp.float32) * scale
        logits = logits + ci.transpose(0, 2, 1)[..., None] - c_k[:, :, None, :]
        q_pos = start + jnp.arange(BLOCK, dtype=jnp.int32)
        causal = k_pos[None, :] <= q_pos[:, None]
        logits = jnp.where(causal[None, None], logits, NEG_INF)
        p = jax.nn.softmax(logits, axis=-1)
        return jnp.einsum('bhqk,bkhd->bqhd', p.astype(v.dtype), v)

    out = lax.map(one_block, (q_blocks, c_blocks, starts))
    return out.transpose(1, 0, 2, 3, 4).reshape(b, s, FOX_HEADS * HEAD_DIM)


def t5_bucket(n):
    n = jnp.maximum(n, 0)
    max_exact = N_BUCKETS // 2
    large = max_exact + (
        jnp.log(jnp.maximum(n, 1).astype(jnp.float32) / max_exact)
        / math.log(MAX_DISTANCE / max_exact) * (N_BUCKETS - max_exact)).astype(jnp.int32)
    large = jnp.minimum(large, N_BUCKETS - 1)
    return jnp.where(n < max_exact, n, large)


def swa_sink_branch(q, k, v, rel_bias, sink):
    b, s = q.shape[0], q.shape[1]
    nb = s // BLOCK
    qb = q.reshape(b, nb, BLOCK, SWA_KV_HEADS, SWA_GROUP, HEAD_DIM)
    kb = k.reshape(b, nb, BLOCK, SWA_KV_HEADS, HEAD_DIM)
    vb = v.reshape(b, nb, BLOCK, SWA_KV_HEADS, HEAD_DIM)

    def band(t):
        prev = jnp.concatenate([jnp.zeros_like(t[:, :1]), t[:, :-1]], axis=1)
        return jnp.concatenate([prev, t], axis=2)

    k_band, v_band = band(kb), band(vb)
    tq = jnp.arange(BLOCK, dtype=jnp.int32)
    sk = jnp.arange(2 * BLOCK, dtype=jnp.int32)
    dist = BLOCK + tq[:, None] - sk[None, :]
    in_window = (dist >= 0) & (dist < WINDOW)
    key_pos = jnp.arange(nb, dtype=jnp.int32)[:, None] * BLOCK - BLOCK + sk[None, :]
    mask = in_window[None] & (key_pos >= 0)[:, None, :]
    bias = rel_bias.astype(jnp.float32)[t5_bucket(dist)]
    bias = bias.reshape(BLOCK, 2 * BLOCK, SWA_KV_HEADS, SWA_GROUP).transpose(2, 3, 0, 1)
    logits = jnp.einsum('bnqkgd,bnskd->bnkgqs', qb, k_band).astype(jnp.float32) * HEAD_DIM ** -0.5
    logits = jnp.where(mask[None, :, None, None], logits + bias, NEG_INF)
    sink_l = sink.astype(jnp.float32).reshape(SWA_KV_HEADS, SWA_GROUP)[None, None, :, :, None]
    m = jnp.maximum(logits.max(axis=-1), sink_l)
    p = jnp.exp(logits - m[..., None])
    denom = p.sum(axis=-1) + jnp.exp(sink_l - m)
    p = p / denom[..., None]
    out = jnp.einsum('bnkgqs,bnskd->bnqkgd', p.astype(v.dtype), v_band)
    return out.reshape(b, s, SWA_HEADS * HEAD_DIM)


def cross_attention(xn, mem_n, w_q, w_kv, w_o):
    b, s = xn.shape[0], xn.shape[1]
    q = (xn @ w_q).reshape(b, s, X_HEADS, X_HEAD_DIM)
    kv = mem_n @ w_kv
    k = kv[..., :X_HEADS * X_HEAD_DIM].reshape(b, -1, X_HEADS, X_HEAD_DIM)
    v = kv[..., X_HEADS * X_HEAD_DIM:].reshape(b, -1, X_HEADS, X_HEAD_DIM)
    logits = jnp.einsum('bshd,bmhd->bhsm', q, k).astype(jnp.float32) * X_HEAD_DIM ** -0.5
    p = jax.nn.softmax(logits, axis=-1)
    o = jnp.einsum('bhsm,bmhd->bshd', p.astype(v.dtype), v).reshape(b, s, X_HEADS * X_HEAD_DIM)
    return o @ w_o


def swiglu(xn, w_gate, w_up, w_down):
    return (jax.nn.silu(xn @ w_gate) * (xn @ w_up)) @ w_down


def setup_inputs(seed: int = 0) -> dict:
    key = jax.random.key(seed)
    ks = jax.random.split(key, 22)
    f32 = jnp.float32

    def nrm(k, shape, scale):
        return scale * jax.random.normal(k, shape, f32)

    def gain(k, shape):
        return 1.0 + 0.1 * jax.random.normal(k, shape, f32)

    return {
        "x": nrm(ks[0], (BATCH, SEQ, D_MODEL), 1.0),
        "mem": nrm(ks[1], (BATCH, MEM_LEN, D_MODEL), 1.0),
        "mix_norm_g": gain(ks[2], (DEPTH, D_MODEL)),
        "w_in": nrm(ks[3], (DEPTH, D_MODEL, IN_COLS), D_MODEL ** -0.5),
        "forget_bias": 4.0 + 0.5 * jax.random.normal(ks[4], (DEPTH, FOX_HEADS), f32),
        "conv_w": nrm(ks[5], (DEPTH, CONV_K, CONV_WIDTH), CONV_K ** -0.5),
        "sink": nrm(ks[6], (DEPTH, SWA_HEADS), 0.5),
        "w_branch": nrm(ks[7], (DEPTH, N_BRANCH, BRANCH_WIDTH, D_MODEL), BRANCH_WIDTH ** -0.5),
        "w_mix_out": nrm(ks[8], (DEPTH, D_MODEL, D_MODEL), D_MODEL ** -0.5),
        "rel_bias": nrm(ks[9], (N_BUCKETS, SWA_HEADS), 0.5),
        "xattn_norm_g": gain(ks[10], (DEPTH, D_MODEL)),
        "mem_norm_g": gain(ks[11], (DEPTH, D_MODEL)),
        "w_xq": nrm(ks[12], (DEPTH, D_MODEL, X_HEADS * X_HEAD_DIM), D_MODEL ** -0.5),
        "w_xkv": nrm(ks[13], (DEPTH, D_MODEL, 2 * X_HEADS * X_HEAD_DIM), D_MODEL ** -0.5),
        "w_xo": nrm(ks[14], (DEPTH, X_HEADS * X_HEAD_DIM, D_MODEL), (X_HEADS * X_HEAD_DIM) ** -0.5),
        "ffn_norm_g": gain(ks[15], (DEPTH, D_MODEL)),
        "w_ffn_gate": nrm(ks[16], (DEPTH, D_MODEL, D_FF), D_MODEL ** -0.5),
        "w_ffn_up": nrm(ks[17], (DEPTH, D_MODEL, D_FF), D_MODEL ** -0.5),
        "w_ffn_down": nrm(ks[18], (DEPTH, D_FF, D_MODEL), D_FF ** -0.5),
        "final_norm_g": gain(ks[19], (D_MODEL,)),
    }


def reference(x, mem, mix_norm_g, w_in, forget_bias, conv_w, sink, w_branch, w_mix_out,
              rel_bias, xattn_norm_g, mem_norm_g, w_xq, w_xkv, w_xo, ffn_norm_g,
              w_ffn_gate, w_ffn_up, w_ffn_down, final_norm_g):
    b, s = x.shape[0], x.shape[1]
    for l in range(DEPTH):
        h = rms_norm(x, mix_norm_g[l])
        (c_b, c_c, c_u, f_q, f_k, f_v, f_g, s_q, s_k, s_v, gate_logits) = split_proj(h @ w_in[l])
        y_conv = short_conv_branch(c_b, c_c, c_u, conv_w[l])
        y_fox = fox_branch(f_q, f_k, f_v, f_g, forget_bias[l])
        y_swa = swa_sink_branch(s_q, s_k, s_v, rel_bias, sink[l])
        gates = jax.nn.sigmoid(gate_logits.reshape(b, s, N_BRANCH, D_MODEL))
        merged = (gates[:, :, 0] * (y_conv @ w_branch[l, 0])
                  + gates[:, :, 1] * (y_fox @ w_branch[l, 1])
                  + gates[:, :, 2] * (y_swa @ w_branch[l, 2]))
        x = x + merged @ w_mix_out[l]
        x = x + cross_attention(rms_norm(x, xattn_norm_g[l]), rms_norm(mem, mem_norm_g[l]),
                                w_xq[l], w_xkv[l], w_xo[l])
        x = x + swiglu(rms_norm(x, ffn_norm_g[l]), w_ffn_gate[l], w_ffn_up[l], w_ffn_down[l])
    return rms_norm(x, final_norm_g)
```

```python
import numpy as np
import ml_dtypes
from contextlib import ExitStack
import concourse.bass as bass
import concourse.mybir as mybir
from concourse.bass_utils import run_bass_kernel_spmd

F32 = mybir.dt.float32
BF16 = mybir.dt.bfloat16
AF = mybir.ActivationFunctionType
ALU = mybir.AluOpType

D = 1024
T = 8192
NB = 64
NT = 16
NPREV_T = 8
DFF = 2816
NFF = 22
INC = 6920
EPS = 1e-6
NEG = -30000.0
DEBUG_TILES = 0
DBG = set()

C_B, C_C, C_U, C_FQ, C_FK, C_FV, C_FG, C_SQ, C_SK, C_SV, C_G = 0, 512, 1024, 1536, 2048, 2560, 3072, 3080, 3592, 3720, 3848

ENGS = ("pe", "act", "dve", "pool", "sp")


class Buf:
    __slots__ = ("w", "r", "sem", "name")

    def __init__(self, name=""):
        self.w = None
        self.r = []
        self.sem = None
        self.name = name


class _Rec:
    def __getattr__(self, name):
        def f(*a, **k):
            return (name, a, k)
        return f


_REC = _Rec()


class Prog:
    def __init__(self, nc, esets, dhw_sets, dsw):
        self.nc = nc
        self.esets = esets
        self.dhw_sets = dhw_sets
        self.dsw = list(dsw)
        self.swids = set(id(s) for s in self.dsw)
        self.to_clear = []
        self.reset()

    def reset(self):
        self.phase = getattr(self, "phase", 0) + 1
        cur = self.phase % 2
        self.esem = dict(zip(ENGS, self.esets[cur]))
        self.ops = {e: [] for e in ENGS}
        self.ecount = {e: 0 for e in ENGS}
        self.dcount = {s: v for s, v in getattr(self, "dcount", {}).items() if id(s) in self.swids}
        self.dfree = {"hw": list(self.dhw_sets[cur]), "sw": list(self.dsw)}
        self.known = {e: {} for e in ENGS}

    def _dsem(self, buf, eng):
        kind = "sw" if eng == "pool" else "hw"
        if buf.sem is None or buf.sem[0] != self.phase:
            buf.sem = (self.phase, {})
        d = buf.sem[1]
        if kind not in d:
            s = self.dfree[kind].pop()
            d[kind] = s
            if kind == "hw" or s not in self.dcount:
                self.dcount[s] = 0
        return d[kind]

    def _waits(self, eng, reads, writes):
        w = {}

        def add(ev):
            if ev is None:
                return
            s, v, src, ph = ev
            if ph != self.phase:
                return
            if src == eng and eng == "pe":
                return
            if self.known[eng].get(s, 0) >= v:
                return
            if w.get(s, (0,))[0] < v:
                w[s] = (v,)

        for b in reads:
            add(b.w)
        for b in writes:
            add(b.w)
            for ev in b.r:
                add(ev)
        out = []
        for s, (v,) in w.items():
            self.known[eng][s] = v
            out.append((s, v))
        return out

    def op(self, eng, fn, reads=(), writes=(), sig=True):
        name_, a_, k_ = fn(_REC)
        fn = (lambda e, name_=name_, a_=a_, k_=k_: getattr(e, name_)(*a_, **k_))
        waits = self._waits(eng, reads, writes)
        if sig:
            self.ecount[eng] += 1
            ev = (self.esem[eng], self.ecount[eng], eng, self.phase)
        else:
            ev = None
        self.ops[eng].append((waits, fn, (self.esem[eng], 1) if sig else None))
        if ev is not None:
            for b in reads:
                b.r.append(ev)
            for b in writes:
                b.w = ev
                b.r = []
        return ev

    def dma(self, eng, pairs, sbuf, reads=(), writes=()):
        waits = self._waits(eng, reads, writes)
        s = self._dsem(sbuf, eng)
        self.dcount[s] += 16 * len(pairs)
        ev = (s, self.dcount[s], "dma", self.phase)

        def fn(e, pairs=pairs):
            return [e.dma_start(out=o, in_=i) for (o, i) in pairs]

        self.ops[eng].append((waits, fn, (s, 16)))
        for b in reads:
            b.r.append(ev)
        for b in writes:
            b.w = ev
            b.r = []
        return ev

    def run_phase(self, name):
        nc = self.nc
        final_waits = [(s, v) for s, v in self.dcount.items() if v > 0]
        used = [s for s, v in self.dcount.items() if v > 0 and id(s) not in self.swids] + [self.esem[e] for e in ENGS if self.ecount[e] > 0]
        ops = self.ops
        to_clear = self.to_clear
        with nc.Block() as block:
            def mk(ename):
                def body(eng):
                    for waits, fn, inc in ops[ename]:
                        for s, v in waits:
                            eng.wait_ge(s, v)
                        ins = fn(eng)
                        if inc is not None:
                            if isinstance(ins, list):
                                for i in ins:
                                    i.then_inc(inc[0], inc[1])
                            else:
                                ins.then_inc(inc[0], inc[1])
                    if ename == "pool":
                        for s, v in final_waits:
                            eng.wait_ge(s, v)
                        for s in to_clear:
                            eng.sem_clear(s)
                return body
            block.tensor(mk("pe"))
            block.scalar(mk("act"))
            block.vector(mk("dve"))
            block.gpsimd(mk("pool"))
            block.sync(mk("sp"))
        self.to_clear = used
        self.reset()


def build_program(debug_outs=(), n_layers=2, stop_after=None):
    nc = bass.Bass("TRN2", target_bir_lowering=False)
    dbg = set(debug_outs)

    def din(name, shape, dt=F32):
        return nc.dram_tensor(name, list(shape), dt, kind="ExternalInput").ap()

    def dscr(name, shape, dt):
        kind = "ExternalOutput" if name in dbg else "Internal"
        return nc.dram_tensor(name, list(shape), dt, kind=kind).ap()

    x_in = din("x", [T, D])
    mem_in = din("mem", [256, D])
    valid_in = din("valid", [128, NB])
    cf_in = din("cf", [128, 128 * 3 + 4 * 128])
    cb_in = din("cb", [128, 128 * 3 + 16], BF16)
    swab_in = din("swab", [128, 8, 256])
    w_in = din("w_in", [2, D, INC])
    w_branch = din("w_branch", [2, 3, 512, D])
    w_mix_out = din("w_mix_out", [2, D, D])
    w_xq = din("w_xq", [2, D, D])
    w_xkv = din("w_xkv", [2, D, 2 * D])
    w_xo = din("w_xo", [2, D, D])
    w_fg = din("w_ffn_gate", [2, D, DFF])
    w_fu = din("w_ffn_up", [2, D, DFF])
    w_fd = din("w_ffn_down", [2, DFF, D])
    gains = din("gains", [9, D])
    fbias = din("forget_bias", [2, 8])
    sink_in = din("sink", [2, 8])
    convw = din("conv_wT", [2, 512, 3])
    out_d = nc.dram_tensor("out", [T // 2, D], F32, kind="ExternalOutput").ap()

    xres = dscr("xres", [T, D], F32)
    hT_d = dscr("hT_d", [8, 128, T], BF16)
    ycv_d = dscr("ycv_d", [4, 128, T], BF16)
    yfx_d = dscr("yfx_d", [4, 128, T], BF16)
    ysw_d = dscr("ysw_d", [4, 128, T], BF16)
    fq_d = dscr("fq_d", [4, 128, T], BF16)
    fk_d = dscr("fk_d", [4, 128, T], BF16)
    sq_d = dscr("sq_d", [4, 128, T], BF16)
    sk_d = dscr("sk_d", [128, T], BF16)
    fv_d = dscr("fv_d", [8, 128, NB, 65], BF16)
    sv_d = dscr("sv_d", [2, 128, NB, 65], BF16)
    negc_d = dscr("negc_d", [128, NB * 8], F32)
    aT_d = dscr("aT_d", [NFF, 128, T], BF16)

    with ExitStack() as top:
        def sb(name, shape, dt, st=top):
            return st.enter_context(nc.sbuf_tensor(name, list(shape), dt))

        def ps(name, shape, dt, st):
            return st.enter_context(nc.psum_tensor(name, list(shape), dt))

        esets = [[top.enter_context(nc.semaphore("e%d_%s" % (k, e))) for e in ENGS] for k in range(2)]
        dhw_sets = [[top.enter_context(nc.semaphore("h%d_%d" % (k, i))) for i in range(30)] for k in range(2)]
        dsw = [top.enter_context(nc.semaphore("s%d" % i)) for i in range(24)]
        P = Prog(nc, esets, dhw_sets, dsw)

        cf = sb("cf_sb", [128, 128 * 3 + 4 * 128], F32)
        cb = sb("cb_sb", [128, 128 * 3 + 16], BF16)
        valid = sb("valid_sb", [128, NB], F32)
        fb_sb = sb("fb_sb", [128, 2, 8], F32)
        esink = sb("esink_sb", [128, 2, 8], F32)
        cw_sb = sb("cw_sb", [128, 2, 4, 3], F32)
        flog = sb("flog_sb", [128, NB, 8], F32)
        negc = sb("negc_sb", [128, NB, 8], F32)
        negc0 = sb("negc0_sb", [128, NB, 8], F32)
        B_const = Buf("const")
        B_flog = Buf("flog")
        B_negc = Buf("negc")

        U_f = cf[:, 0:128]
        ones_f = cf[:, 128:256]
        sel0_f = cf[:, 256:384]
        sel64_f = cf[:, 384:448]
        ident_b = cb[:, 0:128]
        tri_b = cb[:, 128:256]
        ones_f_b = cb[:, 272:400]

        P.dma("sp", [(cf[:], cf_in[:, :]), (cb[:], cb_in[:, :]), (valid[:], valid_in[:, :])], B_const, writes=[B_const])
        P.dma("sp", [(fb_sb[:, i, :], fbias[i:i + 1, :].partition_broadcast(128)) for i in range(2)]
              + [(esink[:, i, :], sink_in[i:i + 1, :].partition_broadcast(128)) for i in range(2)]
              + [(cw_sb[:, i, :, :], convw[i].rearrange("(c p) k -> p c k", p=128)) for i in range(2)],
              B_const, writes=[B_const])
        P.op("act", lambda e: e.activation(out=esink[:], in_=esink[:], func=AF.Exp), reads=[B_const], writes=[B_const])
        P.op("pool", lambda e: e.memset(flog[:], 0.0), writes=[B_flog])
        P.run_phase("const")
        if stop_after == ("0",):
            return nc

        gcount = [0]

        def gain_tile(st, gi):
            gcount[0] += 1
            gt = sb("gain%d" % gcount[0], [128, D], F32, st)
            B_g = Buf()
            P.dma("sp", [(gt[:], gains[gi:gi + 1, :].partition_broadcast(128))], B_g, writes=[B_g])
            return gt, B_g

        def rmsnorm_block(P, st_bufs, x_ap, gi, h_out, B_x, B_h, tmp):
            junk, ss, B_tmp = tmp
            P.op("act", lambda e: e.activation(out=junk[:], in_=x_ap, func=AF.Square, accum_out=ss[:, 0:1]),
                 reads=[B_x], writes=[B_tmp])
            P.op("act", lambda e: e.activation(out=ss[:, 1:2], in_=ss[:, 0:1], func=AF.Sqrt, bias=EPS, scale=1.0 / D),
                 reads=[B_tmp], writes=[B_tmp])
            P.op("dve", lambda e: e.reciprocal(out=ss[:, 2:3], in_=ss[:, 1:2]), reads=[B_tmp], writes=[B_tmp])
            P.op("dve", lambda e: e.scalar_tensor_tensor(out=h_out, in0=x_ap, scalar=ss[:, 2:3], in1=gi[0][:],
                                                         op0=ALU.mult, op1=ALU.mult),
                 reads=[B_x, B_tmp, gi[1]], writes=[B_h])

        def transpose_block(P, h_ap, B_h, pt, B_pt, hT_dst, B_hT, eng="act"):
            for kc in range(8):
                P.op("pe", lambda e, kc=kc: e.transpose(out=pt[:, kc, :], in_=h_ap[:, kc * 128:(kc + 1) * 128], identity=ident_b),
                     reads=[B_h, B_const], writes=[B_pt], sig=(kc == 7))
            if eng == "act":
                P.op("act", lambda e: e.copy(out=hT_dst, in_=pt[:]), reads=[B_pt], writes=[B_hT])
            else:
                P.op("dve", lambda e: e.tensor_copy(out=hT_dst, in_=pt[:]), reads=[B_pt], writes=[B_hT])

        def load_w(P, dst, B_dst, src2d, ncols, nk=8, rows_per=128):
            pairs = []
            for k in range(nk):
                pairs.append((dst[:, k, :], src2d[k * 128:(k + 1) * 128, :]))
            P.dma("pool", pairs, B_dst, writes=[B_dst])

        class Rot:
            def __init__(self, items):
                self.items = items
                self.i = 0

            def next(self):
                it = self.items[self.i % len(self.items)]
                self.i += 1
                return it

        for l in range(n_layers):
            x_src = x_in if l == 0 else xres
            full_tiles = list(range(NT)) if l == 0 else list(range(NPREV_T, NT))
            a_tiles = list(range(NT))
            a_full = (lambda t: True) if l == 0 else (lambda t: t >= NPREV_T - 1)

            with ExitStack() as st:
                wfm = sb("A_wfm%d" % l, [128, 8, 3200], BF16, st)
                wtk = sb("A_wtk%d" % l, [128, 8, 648], BF16, st)
                B_wfm = [Buf() for _ in range(4)]
                B_wtk = Buf()
                wl = w_in[l]
                P.dma("pool", [(wtk[:, k, 0:512], wl[k * 128:(k + 1) * 128, C_FV:C_FV + 512]) for k in range(8)]
                      + [(wtk[:, k, 512:520], wl[k * 128:(k + 1) * 128, C_FG:C_FG + 8]) for k in range(8)]
                      + [(wtk[:, k, 520:648], wl[k * 128:(k + 1) * 128, C_SV:C_SV + 128]) for k in range(8)],
                      B_wtk, writes=[B_wtk])
                segs = [(0, 0, 1536), (1536, 1536, 1024), (2560, C_SQ, 640)]
                for si, (d0, s0, n) in sorted(enumerate(segs), key=lambda q: (q[0] + 2) % 3):
                    P.dma("pool", [(wfm[:, k, d0:d0 + n], wl[k * 128:(k + 1) * 128, s0:s0 + n]) for k in range(8)],
                          B_wfm[si], writes=[B_wfm[si]])

                def wfm_buf(col):
                    return B_wfm[0] if col < 1536 else (B_wfm[1] if col < 2560 else B_wfm[2])

                gA = gain_tile(st, 0 + l)
                xb = [sb("A_x%d_%d" % (l, i), [128, D], F32, st) for i in range(4)]
                B_xb = [Buf() for _ in range(4)]
                junk = sb("A_junk%d" % l, [128, D], BF16, st)
                ssb = [sb("A_ss%d_%d" % (l, i), [128, 4], F32, st) for i in range(2)]
                B_tmp = [Buf() for _ in range(2)]
                hb = [sb("A_h%d_%d" % (l, i), [128, D], BF16, st) for i in range(4)]
                B_hb = [Buf() for _ in range(4)]
                hT = [sb("A_hT%d_%d" % (l, i), [128, 8, 512], BF16, st) for i in range(2)]
                B_hT = [Buf() for _ in range(2)]
                ptr2 = [ps("A_pt%d_%d" % (l, i), [128, 8, 128], BF16, st) for i in range(1)] * 2
                B_ptr2 = [Buf()] * 2
                pbanks = [ps("A_ps%d_%d" % (l, i), [128, 512], F32, st) for i in range(7)]
                prot = Rot([(pbanks[i], Buf()) for i in range(7)])
                zc = [sb("A_z%d_%d" % (l, c), [128, 514], F32, st) for c in range(4)]
                B_zc = [Buf() for _ in range(4)]
                usb2 = [sb("A_u%d_%d" % (l, i), [128, 512], F32, st) for i in range(2)]
                B_usb2 = [Buf() for _ in range(2)]
                acc4 = [sb("A_acc%d_%d" % (l, i), [128, 512], F32, st) for i in range(4)]
                B_acc4 = [Buf() for _ in range(4)]
                ycv = [sb("A_ycv%d_%d" % (l, i), [128, 4, 512], BF16, st) for i in range(2)]
                B_ycv = [Buf() for _ in range(2)]
                stg = [sb("A_stg%d_%d" % (l, i), [128, 4, 512], BF16, st) for i in range(3)]
                srot = Rot([(stg[i], Buf()) for i in range(3)])
                vt = [sb("A_vt%d_%d" % (l, i), [128, 8, 4, 65], BF16, st) for i in range(2)]
                B_vt = [Buf() for _ in range(2)]
                svt = [sb("A_svt%d_%d" % (l, i), [128, 2, 4, 65], BF16, st) for i in range(2)]
                B_svt = [Buf() for _ in range(2)]

                for c in range(4):
                    P.op("pool", lambda e, c=c: e.memset(zc[c][:], 0.0), writes=[B_zc[c]])

                evac_i = [0]

                def evac_copy(P, out_ap, in_ap, reads, writes):
                    evac_i[0] += 1
                    if evac_i[0] % 2 == 0:
                        P.op("act", lambda e: e.copy(out=out_ap, in_=in_ap), reads=reads, writes=writes)
                    else:
                        P.op("dve", lambda e: e.tensor_copy(out=out_ap, in_=in_ap), reads=reads, writes=writes)

                def fm_chunk(P, col, hTt, B_hTt):
                    pb, B_pb = prot.next()
                    for kc in range(8):
                        P.op("pe", lambda e, kc=kc, pb=pb: e.matmul(pb[:], lhsT=wfm[:, kc, col:col + 128], rhs=hTt[:, kc, :],
                                                                   start=(kc == 0), stop=(kc == 7)),
                             reads=[wfm_buf(col), B_hTt], writes=[B_pb], sig=(kc == 7))
                    return pb, B_pb

                def a_norm_ew(ti):
                    t = a_tiles[ti]
                    for j in range(4):
                        blk = t * 4 + j
                        P.dma("sp", [(xb[j][:], x_src[blk * 128:(blk + 1) * 128, :])], B_xb[j], writes=[B_xb[j]])
                        rmsnorm_block(P, None, xb[j][:], gA, hb[j][:], B_xb[j], B_hb[j], (junk, ssb[j % 2], B_tmp[j % 2]))

                def a_norm_tr(ti):
                    t = a_tiles[ti]
                    full = a_full(t)
                    s = ti % 2
                    hTt, B_hTt = hT[s], B_hT[s]
                    for j in range(4):
                        transpose_block(P, hb[j], B_hb[j], ptr2[j % 2], B_ptr2[j % 2], hTt[:, :, j * 128:(j + 1) * 128], B_hTt,
                                        eng=("act" if j % 2 == 0 else "dve"))
                    if full:
                        P.dma("sp", [(hT_d[:, :, t * 512:(t + 1) * 512].rearrange("k p t -> p k t"), hTt[:])], B_hTt, reads=[B_hTt])
                def a_main(ti):
                    t = a_tiles[ti]
                    full = a_full(t)
                    s = ti % 2
                    hTt, B_hTt = hT[s], B_hT[s]
                    vs = ti % 2
                    for j in range(4):
                        blk = t * 4 + j
                        pv, B_pv = prot.next()
                        for kc in range(8):
                            P.op("pe", lambda e, kc=kc, pv=pv, j=j: e.matmul(pv[:], lhsT=hTt[:, kc, j * 128:(j + 1) * 128], rhs=wtk[:, kc, 0:512],
                                                                            start=(kc == 0), stop=(kc == 7)),
                                 reads=[B_wtk, B_hTt], writes=[B_pv], sig=(kc == 7))
                        pg, B_pg = prot.next()
                        for kc in range(8):
                            P.op("pe", lambda e, kc=kc, pg=pg, j=j: e.matmul(pg[:, 0:136], lhsT=hTt[:, kc, j * 128:(j + 1) * 128], rhs=wtk[:, kc, 512:648],
                                                                            start=(kc == 0), stop=(kc == 7)),
                                 reads=[B_wtk, B_hTt], writes=[B_pg], sig=(kc == 7))
                        vcol = valid[:, blk:blk + 1]
                        P.op("dve", lambda e, pv=pv, j=j, vcol=vcol: e.tensor_scalar(
                            out=vt[vs][:, :, j, 0:64], in0=pv[:].rearrange("p (h d) -> p h d", h=8), scalar1=vcol, scalar2=None, op0=ALU.mult),
                            reads=[B_pv, B_const], writes=[B_vt[vs]])
                        P.op("pool", lambda e, j=j, vcol=vcol: e.tensor_copy(out=vt[vs][:, :, j, 64:65], in_=vcol.unsqueeze(1).broadcast_to([128, 8, 1])),
                             reads=[B_const], writes=[B_vt[vs]])
                        P.op("dve", lambda e, pg=pg, j=j, vcol=vcol: e.tensor_scalar(
                            out=svt[vs][:, :, j, 0:64], in0=pg[:, 8:136].rearrange("p (h d) -> p h d", h=2), scalar1=vcol, scalar2=None, op0=ALU.mult),
                            reads=[B_pg, B_const], writes=[B_svt[vs]])
                        P.op("pool", lambda e, j=j, vcol=vcol: e.tensor_copy(out=svt[vs][:, :, j, 64:65], in_=vcol.unsqueeze(1).broadcast_to([128, 2, 1])),
                             reads=[B_const], writes=[B_svt[vs]])
                        P.op("dve", lambda e, pg=pg, blk=blk: e.tensor_tensor(out=flog[:, blk, :], in0=pg[:, 0:8], in1=fb_sb[:, l, :], op=ALU.add),
                             reads=[B_pg, B_const], writes=[B_flog])
                    P.dma("pool", [(fv_d[:, :, t * 4:(t + 1) * 4, :].rearrange("h p b d -> p h b d"), vt[vs][:])], B_vt[vs], reads=[B_vt[vs]])
                    P.dma("pool", [(sv_d[:, :, t * 4:(t + 1) * 4, :].rearrange("h p b d -> p h b d"), svt[vs][:])], B_svt[vs], reads=[B_svt[vs]])
                    groups = [("fk", 2048, fk_d, 4)]
                    if full:
                        groups = [("fq", 1536, fq_d, 4), ("fk", 2048, fk_d, 4), ("sq", 2560, sq_d, 4)]
                    for (nm, c0, dst, nchunk) in groups:
                        sg, B_sg = srot.next()
                        for c in range(nchunk):
                            pb, B_pb = fm_chunk(P, c0 + c * 128, hTt, B_hTt)
                            evac_copy(P, sg[:, c, :], pb[:], [B_pb], [B_sg])
                        P.dma("sp", [(dst[:, :, t * 512:(t + 1) * 512].rearrange("c p t -> p c t"), sg[:])], B_sg, reads=[B_sg])
                    sg, B_sg = srot.next()
                    pb, B_pb = fm_chunk(P, 3072, hTt, B_hTt)
                    evac_copy(P, sg[:, 0, :], pb[:], [B_pb], [B_sg])
                    P.dma("sp", [(sk_d[:, t * 512:(t + 1) * 512], sg[:, 0, :])], B_sg, reads=[B_sg])
                    if full:
                        ys = ti % 2
                        for c in range(4):
                            pB, B_pB = fm_chunk(P, 0 + c * 128, hTt, B_hTt)
                            pC, B_pC = fm_chunk(P, 512 + c * 128, hTt, B_hTt)
                            pU, B_pU = fm_chunk(P, 1024 + c * 128, hTt, B_hTt)
                            z = zc[c]
                            P.op("pool", lambda e, z=z: e.tensor_copy(out=z[:, 0:2], in_=z[:, 512:514]), reads=[B_zc[c]], writes=[B_zc[c]])
                            if t == NPREV_T:
                                P.op("pool", lambda e, z=z: e.tensor_scalar(out=z[:, 0:2], in0=z[:, 0:2], scalar1=valid[:, NPREV_T * 4 - 1:NPREV_T * 4],
                                                                            scalar2=None, op0=ALU.mult),
                                     reads=[B_zc[c], B_const], writes=[B_zc[c]])
                            usb, B_usb = usb2[c % 2], B_usb2[c % 2]
                            a0, a1 = acc4[(c % 2) * 2], acc4[(c % 2) * 2 + 1]
                            B_a0, B_a1 = B_acc4[(c % 2) * 2], B_acc4[(c % 2) * 2 + 1]
                            P.op("act", lambda e: e.copy(out=usb[:], in_=pU[:]), reads=[B_pU], writes=[B_usb])
                            P.op("dve", lambda e: e.tensor_tensor(out=z[:, 2:514], in0=pC[:], in1=usb[:], op=ALU.mult),
                                 reads=[B_pC, B_usb], writes=[B_zc[c]])
                            P.op("act", lambda e: e.activation(out=a0[:], in_=z[:, 2:514], func=AF.Copy, scale=cw_sb[:, l, c, 2:3]),
                                 reads=[B_zc[c], B_const], writes=[B_a0])
                            P.op("dve", lambda e: e.scalar_tensor_tensor(out=a1[:], in0=z[:, 1:513], scalar=cw_sb[:, l, c, 1:2], in1=a0[:],
                                                                         op0=ALU.mult, op1=ALU.add),
                                 reads=[B_zc[c], B_const, B_a0], writes=[B_a1])
                            P.op("dve", lambda e: e.scalar_tensor_tensor(out=a0[:], in0=z[:, 0:512], scalar=cw_sb[:, l, c, 0:1], in1=a1[:],
                                                                         op0=ALU.mult, op1=ALU.add),
                                 reads=[B_zc[c], B_const, B_a1], writes=[B_a0])
                            P.op("dve", lambda e: e.tensor_tensor(out=ycv[ys][:, c, :], in0=pB[:], in1=a0[:], op=ALU.mult),
                                 reads=[B_pB, B_a0], writes=[B_ycv[ys]])
                        P.dma("sp", [(ycv_d[:, :, t * 512:(t + 1) * 512].rearrange("c p t -> p c t"), ycv[ys][:])], B_ycv[ys], reads=[B_ycv[ys]])

                a_norm_ew(0)
                a_norm_tr(0)
                for ti in range(len(a_tiles)):
                    if ti + 1 < len(a_tiles):
                        a_norm_ew(ti + 1)
                    a_main(ti)
                    if ti + 1 < len(a_tiles):
                        a_norm_tr(ti + 1)
                P.run_phase("A%d" % l)
            if stop_after == ("A", l):
                break

            with ExitStack() as st:
                lsb = sb("B_l%d" % l, [128, NB, 8], F32, st)
                rsb = sb("B_r%d" % l, [128, NB, 8], F32, st)
                B_l, B_r = Buf(), Buf()
                pc = ps("B_pc%d" % l, [128, 512], F32, st)
                pc0 = ps("B_pc0%d" % l, [128, 512], F32, st)
                B_pc, B_pc0 = Buf(), Buf()
                P.op("act", lambda e: e.activation(out=lsb[:], in_=flog[:], func=AF.Exp, scale=-1.0), reads=[B_flog], writes=[B_l])
                P.op("act", lambda e: e.activation(out=lsb[:], in_=lsb[:], func=AF.Ln, bias=1.0, scale=1.0), reads=[B_l], writes=[B_l])
                P.op("dve", lambda e: e.memset(rsb[:, 0, :], 0.0), writes=[B_r])
                for b in range(1, NB):
                    P.op("dve", lambda e, b=b: e.tensor_tensor(out=rsb[:, b, :], in0=rsb[:, b - 1, :], in1=lsb[:, b - 1, :], op=ALU.add),
                         reads=[B_l, B_r], writes=[B_r])
                P.op("pe", lambda e: e.matmul(pc[:], lhsT=U_f, rhs=lsb[:].rearrange("p b h -> p (b h)"), start=True, stop=False),
                     reads=[B_l, B_const], writes=[B_pc], sig=False)
                P.op("pe", lambda e: e.matmul(pc[:], lhsT=ones_f, rhs=rsb[:].rearrange("p b h -> p (b h)"), start=False, stop=True),
                     reads=[B_r, B_const], writes=[B_pc])
                P.op("dve", lambda e: e.tensor_copy(out=negc[:].rearrange("p b h -> p (b h)"), in_=pc[:]), reads=[B_pc], writes=[B_negc])
                P.op("pe", lambda e: e.matmul(pc0[:], lhsT=sel0_f, rhs=negc[:].rearrange("p b h -> p (b h)"), start=True, stop=True),
                     reads=[B_negc, B_const], writes=[B_pc0])
                P.op("dve", lambda e: e.tensor_copy(out=negc0[:].rearrange("p b h -> p (b h)"), in_=pc0[:]), reads=[B_pc0], writes=[B_negc])
                if "negc_d" in dbg:
                    P.dma("sp", [(negc_d[:, :], negc[:].rearrange("p b h -> p (b h)"))], B_negc, reads=[B_negc])
                P.run_phase("B%d" % l)
            if stop_after == ("B", l):
                break

            g_list = list(range(16)) if l == 0 else list(range(8, 16))
            if DEBUG_TILES:
                g_list = g_list[:DEBUG_TILES]
                full_tiles = full_tiles[:DEBUG_TILES]

            def attn_finish(P, ya, B_ya, rr, B_rr, rb, B_rb, rbs, B_rbs, out_ap, B_out, sink_ap=None):
                if sink_ap is not None:
                    P.op("dve", lambda e: e.tensor_scalar(out=rr[64:65, :], in0=ya[64:65, :], scalar1=sink_ap, scalar2=1e-30, op0=ALU.add, op1=ALU.max),
                         reads=[B_ya, B_const], writes=[B_rr])
                else:
                    P.op("dve", lambda e: e.tensor_scalar(out=rr[64:65, :], in0=ya[64:65, :], scalar1=1e-30, scalar2=None, op0=ALU.max),
                         reads=[B_ya], writes=[B_rr])
                P.op("dve", lambda e: e.reciprocal(out=rr[64:65, :], in_=rr[64:65, :]), reads=[B_rr], writes=[B_rr])
                P.op("pe", lambda e: e.matmul(rb[0:64, :], lhsT=ones_f[64:65, 0:64], rhs=rr[64:65, :], start=True, stop=True),
                     reads=[B_rr, B_const], writes=[B_rb])
                P.op("act", lambda e: e.copy(out=rbs[:], in_=rb[0:64, :]), reads=[B_rb], writes=[B_rbs])
                P.op("dve", lambda e: e.tensor_tensor(out=out_ap, in0=ya[0:64, :], in1=rbs[:], op=ALU.mult),
                     reads=[B_ya, B_rbs], writes=[B_out])

            with ExitStack() as st:
                kz = [sb("C_kz%d_%d" % (l, i), [128, T], BF16, st) for i in range(2)]
                B_kz = [Buf() for _ in range(2)]
                qp = [sb("C_qp%d_%d" % (l, i), [128, T], BF16, st) for i in range(2)]
                B_qp = [Buf() for _ in range(2)]
                vh = [sb("C_vh%d_%d" % (l, i), [128, NB, 65], BF16, st) for i in range(2)]
                B_vh = [Buf() for _ in range(2)]
                biasg = [sb("C_bg%d_%d" % (l, i), [128, NB], F32, st) for i in range(2)]
                B_bg = [Buf() for _ in range(2)]
                pts = [sb("C_pt%d_%d" % (l, i), [128, 512], BF16, st) for i in range(6)]
                ptrot = Rot([(pts[i], Buf()) for i in range(6)])
                sps = [ps("C_s%d_%d" % (l, i), [128, 512], F32, st) for i in range(5)]
                srot2 = Rot([(sps[i], Buf()) for i in range(5)])
                yac = [ps("C_y%d_%d" % (l, i), [128, 512], F32, st) for i in range(2)]
                B_yac = [Buf() for _ in range(2)]
                rb = ps("C_rb%d" % l, [128, 512], F32, st)
                B_rb = Buf()
                rr = [sb("C_rr%d_%d" % (l, i), [128, 512], F32, st) for i in range(2)]
                B_rr = [Buf() for _ in range(2)]
                rbs = sb("C_rbs%d" % l, [64, 512], F32, st)
                B_rbs = Buf()
                yo = [sb("C_yo%d_%d" % (l, i), [64, 512], BF16, st) for i in range(2)]
                B_yo = [Buf() for _ in range(2)]
                for i_ in range(2):
                    P.op("pool", lambda e: e.memset(rr[i_][:], 0.0), writes=[B_rr[i_]])
                P.op("pool", lambda e: e.memset(kz[0][64:128, :], 0.0), writes=[B_kz[0]])
                P.op("pool", lambda e: e.memset(kz[1][0:64, :], 0.0), writes=[B_kz[1]])
                it = 0
                LA = 3
                PEND_DELAY = 6
                def c_load(h_):
                    hp_, par_ = h_ // 2, h_ % 2
                    if par_ == 0:
                        P.dma("sp", [(qp[hp_ % 2][:, :], fq_d[hp_, :, :])], B_qp[hp_ % 2], writes=[B_qp[hp_ % 2]])
                    P.dma("sp", [(kz[par_][par_ * 64:par_ * 64 + 64, :], fk_d[hp_, par_ * 64:par_ * 64 + 64, :])], B_kz[par_], writes=[B_kz[par_]])
                    P.dma("sp", [(vh[par_][:], fv_d[h_, :, :, :])], B_vh[par_], writes=[B_vh[par_]])

                c_load(0)
                for hp in range(4):
                    qs = hp % 2
                    for par in range(2):
                        h = hp * 2 + par
                        r0 = par * 64
                        if h + 1 < 8:
                            c_load(h + 1)
                        pairs = []
                        for g in g_list:
                            bs = it % 2
                            it += 1
                            for kb in range(4 * g + 4):
                                pairs.append((g, kb, bs))
                        live = {}
                        pend = []

                        def stage1(idx):
                            g, kb, bs = pairs[idx]
                            nk = 4 * g + 4
                            if kb == 0:
                                P.op("pool", lambda e: e.tensor_scalar(
                                    out=biasg[bs][:, 0:nk], in0=negc[:, 0:nk, h], scalar1=negc0[:, 4 * g, h:h + 1], scalar2=None, op0=ALU.subtract),
                                    reads=[B_negc], writes=[B_bg[bs]])
                            jd = kb - 4 * g
                            c0 = max(jd, 0) * 128
                            n = 512 - c0
                            sp_, B_sp = srot2.next()
                            P.op("pe", lambda e: e.matmul(
                                sp_[:, 0:n], lhsT=kz[par][:, kb * 128:(kb + 1) * 128], rhs=qp[qs][:, g * 512 + c0:(g + 1) * 512],
                                start=True, stop=True),
                                reads=[B_kz[par], B_qp[qs]], writes=[B_sp])
                            pt_, B_pt_ = ptrot.next()
                            P.op("act", lambda e: e.activation(
                                out=pt_[:, 0:n], in_=sp_[:, 0:n], func=AF.Exp, bias=biasg[bs][:, kb:kb + 1], scale=0.125),
                                reads=[B_sp, B_bg[bs]], writes=[B_pt_])
                            if jd >= 0:
                                P.op("pool", lambda e: e.tensor_tensor(out=pt_[:, 0:128], in0=pt_[:, 0:128], in1=tri_b, op=ALU.mult),
                                     reads=[B_pt_, B_const], writes=[B_pt_])
                            live[idx] = (pt_, B_pt_, c0, n)

                        def finish_a(g, bs):
                            ya, B_ya = yac[bs], B_yac[bs]
                            P.op("dve", lambda e: e.tensor_scalar(out=rr[bs][64:65, :], in0=ya[64:65, :], scalar1=1e-30, scalar2=None, op0=ALU.max),
                                 reads=[B_ya], writes=[B_rr[bs]])
                            P.op("dve", lambda e: e.reciprocal(out=rr[bs][64:65, :], in_=rr[bs][64:65, :]), reads=[B_rr[bs]], writes=[B_rr[bs]])

                        def finish_b(g, bs):
                            ya, B_ya = yac[bs], B_yac[bs]
                            P.op("pe", lambda e: e.matmul(rb[0:64, :], lhsT=sel64_f, rhs=rr[bs][:, :], start=True, stop=True),
                                 reads=[B_rr[bs], B_const], writes=[B_rb])
                            P.op("act", lambda e: e.copy(out=rbs[:], in_=rb[0:64, :]), reads=[B_rb], writes=[B_rbs])
                            P.op("dve", lambda e: e.tensor_tensor(out=yo[bs][:], in0=ya[0:64, :], in1=rbs[:], op=ALU.mult),
                                 reads=[B_ya, B_rbs], writes=[B_yo[bs]])
                            P.dma("sp", [(yfx_d[hp, r0:r0 + 64, g * 512:(g + 1) * 512], yo[bs][:])], B_yo[bs], reads=[B_yo[bs]])

                        def stage2(idx):
                            g, kb, bs = pairs[idx]
                            nk = 4 * g + 4
                            pt_, B_pt_, c0, n = live.pop(idx)
                            ya, B_ya = yac[bs], B_yac[bs]
                            P.op("pe", lambda e: e.matmul(
                                ya[0:65, c0:512], lhsT=vh[par][:, kb, :], rhs=pt_[:, 0:n], start=(kb == 0), stop=(kb == nk - 1)),
                                reads=[B_vh[par], B_pt_], writes=[B_ya], sig=(kb == nk - 1))
                            if kb == nk - 1:
                                finish_a(g, bs)
                                pend.append([PEND_DELAY, g, bs])

                        for i in range(len(pairs) + LA):
                            if i < len(pairs):
                                stage1(i)
                            if i - LA >= 0:
                                stage2(i - LA)
                            for pf in list(pend):
                                pf[0] -= 1
                                if pf[0] < 0:
                                    finish_b(pf[1], pf[2])
                                    pend.remove(pf)
                        for pf in pend:
                            finish_b(pf[1], pf[2])
                P.run_phase("C%d" % l)
            if stop_after == ("C", l):
                break

            with ExitStack() as st:
                swab = sb("D_swab%d" % l, [128, 8, 256], F32, st)
                B_swab = Buf()
                P.dma("sp", [(swab[:], swab_in[:, :, :])], B_swab, writes=[B_swab])
                P.op("act", lambda e: e.activation(out=swab[:], in_=swab[:], func=AF.Exp), reads=[B_swab], writes=[B_swab])
                kz = [sb("D_kz%d_%d" % (l, i), [128, T], BF16, st) for i in range(2)]
                B_kz = [Buf() for _ in range(2)]
                qp = [sb("D_qp%d_%d" % (l, i), [128, T], BF16, st) for i in range(2)]
                B_qp = [Buf() for _ in range(2)]
                vh = [sb("D_vh%d_%d" % (l, i), [128, NB, 65], BF16, st) for i in range(2)]
                B_vh = [Buf() for _ in range(2)]
                tmpb = [sb("D_tmp%d_%d" % (l, i), [128, 256], F32, st) for i in range(3)]
                trot = Rot([(tmpb[i], Buf()) for i in range(3)])
                pts = [sb("D_pt%d_%d" % (l, i), [128, 256], BF16, st) for i in range(6)]
                ptrot = Rot([(pts[i], Buf()) for i in range(6)])
                sps = [ps("D_s%d_%d" % (l, i), [128, 512], F32, st) for i in range(5)]
                srot2 = Rot([(sps[i], Buf()) for i in range(5)])
                yac = [ps("D_y%d_%d" % (l, i), [128, 512], F32, st) for i in range(2)]
                B_yac = [Buf() for _ in range(2)]
                rb = ps("D_rb%d" % l, [128, 512], F32, st)
                B_rb = Buf()
                rr = [sb("D_rr%d_%d" % (l, i), [128, 512], F32, st) for i in range(2)]
                B_rr = [Buf() for _ in range(2)]
                rbs = sb("D_rbs%d" % l, [64, 512], F32, st)
                B_rbs = Buf()
                yo = [sb("D_yo%d_%d" % (l, i), [64, 512], BF16, st) for i in range(2)]
                B_yo = [Buf() for _ in range(2)]
                for i_ in range(2):
                    P.op("pool", lambda e: e.memset(rr[i_][:], 0.0), writes=[B_rr[i_]])
                P.op("pool", lambda e: e.memset(kz[0][64:128, :], 0.0), writes=[B_kz[0]])
                P.op("pool", lambda e: e.memset(kz[1][0:64, :], 0.0), writes=[B_kz[1]])
                it = 0
                LA = 3
                PEND_DELAY = 3
                def d_load(h_):
                    hp_, par_ = h_ // 2, h_ % 2
                    kvh_ = h_ // 4
                    if par_ == 0:
                        P.dma("sp", [(qp[hp_ % 2][:, :], sq_d[hp_, :, :])], B_qp[hp_ % 2], writes=[B_qp[hp_ % 2]])
                    P.dma("sp", [(kz[par_][par_ * 64:par_ * 64 + 64, :], sk_d[kvh_ * 64:kvh_ * 64 + 64, :])], B_kz[par_], writes=[B_kz[par_]])
                    P.dma("sp", [(vh[par_][:], sv_d[kvh_, :, :, :])], B_vh[par_], writes=[B_vh[par_]])

                d_load(0)
                for hp in range(4):
                    qs = hp % 2
                    for par in range(2):
                        h = hp * 2 + par
                        kvh = h // 4
                        r0 = par * 64
                        if h + 1 < 8:
                            d_load(h + 1)
                        pairs = []
                        for g in g_list:
                            bs = it % 2
                            it += 1
                            kbs = [kb for kb in range(4 * g - 1, 4 * g + 4) if kb >= 0]
                            for ki, kb in enumerate(kbs):
                                pairs.append((g, kb, bs, ki, len(kbs)))
                        live = {}
                        pend = []

                        def stage1(idx):
                            g, kb, bs, ki, nkk = pairs[idx]
                            qlo = max(kb, 4 * g)
                            qhi = min(kb + 1, 4 * g + 3)
                            c0 = (qlo - 4 * g) * 128
                            n = (qhi - qlo + 1) * 128
                            tb0 = 0 if qlo == kb else 128
                            sp_, B_sp = srot2.next()
                            P.op("pe", lambda e: e.matmul(
                                sp_[:, 0:n], lhsT=kz[par][:, kb * 128:(kb + 1) * 128], rhs=qp[qs][:, g * 512 + c0:g * 512 + c0 + n],
                                start=True, stop=True),
                                reads=[B_kz[par], B_qp[qs]], writes=[B_sp])
                            pt_, B_pt_ = ptrot.next()
                            P.op("act", lambda e: e.activation(out=pt_[:, 0:n], in_=sp_[:, 0:n], func=AF.Exp, scale=0.125),
                                 reads=[B_sp], writes=[B_pt_])
                            P.op(("pool" if idx % 2 == 0 else "dve"), lambda e: e.tensor_tensor(out=pt_[:, 0:n], in0=pt_[:, 0:n], in1=swab[:, h, tb0:tb0 + n], op=ALU.mult),
                                 reads=[B_pt_, B_swab], writes=[B_pt_])
                            live[idx] = (pt_, B_pt_, c0, n)

                        def finish_a(g, bs):
                            ya, B_ya = yac[bs], B_yac[bs]
                            P.op("act", lambda e: e.activation(out=rr[bs][64:65, :], in_=ya[64:65, :], func=AF.Ln, bias=esink[64:65, l, h:h + 1]),
                                 reads=[B_ya, B_const], writes=[B_rr[bs]])
                            P.op("act", lambda e: e.activation(out=rr[bs][64:65, :], in_=rr[bs][64:65, :], func=AF.Exp, scale=-1.0),
                                 reads=[B_rr[bs]], writes=[B_rr[bs]])

                        def finish_b(g, bs):
                            ya, B_ya = yac[bs], B_yac[bs]
                            P.op("pe", lambda e: e.matmul(rb[0:64, :], lhsT=sel64_f, rhs=rr[bs][:, :], start=True, stop=True),
                                 reads=[B_rr[bs], B_const], writes=[B_rb])
                            P.op("act", lambda e: e.copy(out=rbs[:], in_=rb[0:64, :]), reads=[B_rb], writes=[B_rbs])
                            P.op("dve", lambda e: e.tensor_tensor(out=yo[bs][:], in0=ya[0:64, :], in1=rbs[:], op=ALU.mult),
                                 reads=[B_ya, B_rbs], writes=[B_yo[bs]])
                            P.dma("sp", [(ysw_d[hp, r0:r0 + 64, g * 512:(g + 1) * 512], yo[bs][:])], B_yo[bs], reads=[B_yo[bs]])

                        def stage2(idx):
                            g, kb, bs, ki, nkk = pairs[idx]
                            pt_, B_pt_, c0, n = live.pop(idx)
                            ya, B_ya = yac[bs], B_yac[bs]
                            P.op("pe", lambda e: e.matmul(
                                ya[0:65, c0:c0 + n], lhsT=vh[par][:, kb, :], rhs=pt_[:, 0:n], start=(ki == 0), stop=(ki == nkk - 1), skip_group_check=True),
                                reads=[B_vh[par], B_pt_], writes=[B_ya], sig=(ki == nkk - 1))
                            if ki == nkk - 1:
                                finish_a(g, bs)
                                pend.append([PEND_DELAY, g, bs])

                        for i in range(len(pairs) + LA):
                            if i < len(pairs):
                                stage1(i)
                            if i - LA >= 0:
                                stage2(i - LA)
                            for pf in list(pend):
                                pf[0] -= 1
                                if pf[0] < 0:
                                    finish_b(pf[1], pf[2])
                                    pend.remove(pf)
                        for pf in pend:
                            finish_b(pf[1], pf[2])
                P.run_phase("D%d" % l)
            if stop_after == ("D", l):
                break

            with ExitStack() as st:
                wg = sb("E_wg%d" % l, [128, 8, 3072], BF16, st)
                wbr = sb("E_wbr%d" % l, [128, 12, D], BF16, st)
                wo = sb("E_wo%d" % l, [128, 8, D], BF16, st)
                B_wgq = [Buf() for _ in range(4)]
                B_wbrq = [Buf() for _ in range(4)]
                B_wo = Buf()
                wbr_src = w_branch[l].rearrange("b r n -> (b r) n")
                for q in range(4):
                    P.dma("pool", [(wg[:, k, b * 1024 + q * 256:b * 1024 + (q + 1) * 256],
                                    w_in[l, k * 128:(k + 1) * 128, C_G + b * 1024 + q * 256:C_G + b * 1024 + (q + 1) * 256]) for b in range(3) for k in range(8)],
                          B_wgq[q], writes=[B_wgq[q]])
                    P.dma("pool", [(wbr[:, k, q * 256:(q + 1) * 256], wbr_src[k * 128:(k + 1) * 128, q * 256:(q + 1) * 256]) for k in range(12)],
                          B_wbrq[q], writes=[B_wbrq[q]])
                P.dma("pool", [(wo[:, k, :], w_mix_out[l, k * 128:(k + 1) * 128, :]) for k in range(8)], B_wo, writes=[B_wo])
                hT = [sb("E_hT%d_%d" % (l, i), [128, 8, 512], BF16, st) for i in range(2)]
                B_hT = [Buf() for _ in range(2)]
                yb = [sb("E_yb%d_%d" % (l, i), [128, 12, 512], BF16, st) for i in range(2)]
                B_yb = [Buf() for _ in range(2)]
                mT = sb("E_mT%d" % l, [128, 8, 512], BF16, st)
                B_mT = Buf()
                sgs = [sb("E_sg%d_%d" % (l, i), [128, 512], F32, st) for i in range(3)]
                sgrot = Rot([(sgs[i], Buf()) for i in range(3)])
                tbs = [sb("E_tb%d_%d" % (l, i), [128, 512], F32, st) for i in range(4)]
                tbrot = Rot([(tbs[i], Buf()) for i in range(4)])
                m01 = sb("E_m01%d" % l, [128, 512], F32, st)
                B_m01 = Buf()
                xb = [sb("E_x%d_%d" % (l, i), [128, D], F32, st) for i in range(2)]
                B_xb = [Buf() for _ in range(2)]
                pbanks = [ps("E_ps%d_%d" % (l, i), [128, 512], F32, st) for i in range(8)]
                prot = Rot([(pbanks[i], Buf()) for i in range(8)])
                ysrc = [ycv_d, yfx_d, ysw_d]
                def e_load(ti):
                    t = full_tiles[ti]
                    s = ti % 2
                    tsl = slice(t * 512, (t + 1) * 512)
                    P.dma("sp", [(hT[s][:], hT_d[:, :, tsl].rearrange("k p t -> p k t"))], B_hT[s], writes=[B_hT[s]])
                    P.dma("sp", [(yb[s][:, b * 4:(b + 1) * 4, :], ysrc[b][:, :, tsl].rearrange("c p t -> p c t")) for b in range(3)],
                          B_yb[s], writes=[B_yb[s]])

                e_load(0)
                for ti, t in enumerate(full_tiles):
                    s = ti % 2
                    if ti + 1 < len(full_tiles):
                        e_load(ti + 1)
                    for fc in ([] if "E_nofc" in DBG else range(8)):
                        tbl = []
                        for b in range(3):
                            pg, B_pg = prot.next()
                            for kc in range(8):
                                P.op("pe", lambda e, pg=pg, kc=kc, b=b, fc=fc, s=s: e.matmul(
                                    pg[:], lhsT=wg[:, kc, b * 1024 + fc * 128:b * 1024 + (fc + 1) * 128], rhs=hT[s][:, kc, :], start=(kc == 0), stop=(kc == 7)),
                                    reads=[B_wgq[fc // 2], B_hT[s]], writes=[B_pg], sig=(kc == 7))
                            pp, B_pp = prot.next()
                            for kc in range(4):
                                P.op("pe", lambda e, pp=pp, kc=kc, b=b, fc=fc, s=s: e.matmul(
                                    pp[:], lhsT=wbr[:, b * 4 + kc, fc * 128:(fc + 1) * 128], rhs=yb[s][:, b * 4 + kc, :], start=(kc == 0), stop=(kc == 3)),
                                    reads=[B_wbrq[fc // 2], B_yb[s]], writes=[B_pp], sig=(kc == 3))
                            sg, B_sg = sgrot.next()
                            P.op("act", lambda e, sg=sg, pg=pg: e.activation(out=sg[:], in_=pg[:], func=(AF.Exp if "E_nosig" in DBG else AF.Sigmoid)), reads=[B_pg], writes=[B_sg])
                            tb, B_tb = tbrot.next()
                            P.op("dve", lambda e, tb=tb, sg=sg, pp=pp: e.tensor_tensor(out=tb[:], in0=pp[:], in1=sg[:], op=ALU.mult),
                                 reads=[B_pp, B_sg], writes=[B_tb])
                            tbl.append((tb, B_tb))
                        aeng = "dve" if "E_nopool" in DBG else "pool"
                        P.op(aeng, lambda e, a=tbl[0][0], b_=tbl[1][0]: e.tensor_tensor(out=m01[:], in0=a[:], in1=b_[:], op=ALU.add),
                             reads=[tbl[0][1], tbl[1][1]], writes=[B_m01])
                        P.op(aeng, lambda e, c_=tbl[2][0], fc=fc: e.tensor_tensor(out=mT[:, fc, :], in0=m01[:], in1=c_[:], op=ALU.add),
                             reads=[B_m01, tbl[2][1]], writes=[B_mT])
                    for j in ([] if "E_noout" in DBG else range(4)):
                        blk = t * 4 + j
                        xs = blk % 2
                        P.dma("sp", [(xb[xs][:], x_src[blk * 128:(blk + 1) * 128, :])], B_xb[xs], writes=[B_xb[xs]])
                        for half in range(2):
                            po, B_po = prot.next()
                            for kc in range(8):
                                P.op("pe", lambda e, po=po, kc=kc, j=j, half=half: e.matmul(
                                    po[:], lhsT=mT[:, kc, j * 128:(j + 1) * 128], rhs=wo[:, kc, half * 512:(half + 1) * 512], start=(kc == 0), stop=(kc == 7)),
                                    reads=[B_mT, B_wo], writes=[B_po], sig=(kc == 7))
                            P.op("dve", lambda e, po=po, xs=xs, half=half: e.tensor_tensor(
                                out=xb[xs][:, half * 512:(half + 1) * 512], in0=po[:], in1=xb[xs][:, half * 512:(half + 1) * 512], op=ALU.add),
                                reads=[B_po, B_xb[xs]], writes=[B_xb[xs]])
                        P.dma("pool", [(xres[blk * 128:(blk + 1) * 128, :], xb[xs][:])], B_xb[xs], reads=[B_xb[xs]])
                P.run_phase("E%d" % l)
            if stop_after == ("E", l):
                break

            with ExitStack() as st:
                wq = sb("F_wq%d" % l, [128, 8, D], BF16, st)
                wkv = sb("F_wkv%d" % l, [128, 8, 2 * D], BF16, st)
                wo = sb("F_wo%d" % l, [128, 8, D], BF16, st)
                B_wq, B_wkv, B_wo = Buf(), Buf(), Buf()
                B_wkvq = [Buf() for _ in range(4)]
                B_wqq = [Buf() for _ in range(2)]
                for q in range(4):
                    P.dma("pool", [(wkv[:, k, q * 512:(q + 1) * 512], w_xkv[l, k * 128:(k + 1) * 128, q * 512:(q + 1) * 512]) for k in range(8)],
                          B_wkvq[q], writes=[B_wkvq[q]])
                for q in range(2):
                    P.dma("pool", [(wq[:, k, q * 512:(q + 1) * 512], w_xq[l, k * 128:(k + 1) * 128, q * 512:(q + 1) * 512]) for k in range(8)],
                          B_wqq[q], writes=[B_wqq[q]])
                P.dma("pool", [(wo[:, k, :], w_xo[l, k * 128:(k + 1) * 128, :]) for k in range(8)], B_wo, writes=[B_wo])
                gM = gain_tile(st, 4 + l)
                gX = gain_tile(st, 2 + l)
                xt = [sb("F_xt%d_%d" % (l, i), [128, 4, D], F32, st) for i in range(2)]
                B_xt = [Buf() for _ in range(2)]
                junk = sb("F_junk%d" % l, [128, D], BF16, st)
                ssb = [sb("F_ss%d_%d" % (l, i), [128, 4], F32, st) for i in range(2)]
                B_tmp = [Buf() for _ in range(2)]
                hb = [sb("F_h%d_%d" % (l, i), [128, D], BF16, st) for i in range(4)]
                B_hb = [Buf() for _ in range(4)]
                hT2 = [sb("F_hT%d_%d" % (l, i), [128, 8, 512], BF16, st) for i in range(2)]
                B_hT2 = [Buf() for _ in range(2)]
                memT = sb("F_memT%d" % l, [128, 8, 256], BF16, st)
                B_memT = Buf()
                kxT = sb("F_kxT%d" % l, [128, 8, 256], BF16, st)
                vx = sb("F_vx%d" % l, [128, 2, D], BF16, st)
                B_kx, B_vx = Buf(), Buf()
                qxT = sb("F_qxT%d" % l, [128, 8, 512], BF16, st)
                B_qx = Buf()
                pTs = [sb("F_pT%d_%d" % (l, i), [128, 512], BF16, st) for i in range(4)]
                B_pT = [Buf() for _ in range(4)]
                yxT = sb("F_yxT%d" % l, [128, 8, 512], BF16, st)
                B_yx = Buf()
                rr = sb("F_rr%d" % l, [128, 512], F32, st)
                B_rr = Buf()
                rbs2 = [sb("F_rbs%d_%d" % (l, i), [128, 512], F32, st) for i in range(2)]
                B_rbs2 = [Buf() for _ in range(2)]
                ptr = ps("F_pt%d" % l, [128, 8, 128], BF16, st)
                B_ptr = Buf()
                pbanks = [ps("F_ps%d_%d" % (l, i), [128, 512], F32, st) for i in range(3)]
                prot = Rot([(pbanks[i], Buf()) for i in range(3)])
                yps = [ps("F_y%d_%d" % (l, i), [128, 512], F32, st) for i in range(4)]
                B_yps = [Buf() for _ in range(4)]
                for mb in range(2):
                    P.dma("sp", [(xt[0][:, mb, :], mem_in[mb * 128:(mb + 1) * 128, :])], B_xt[0], writes=[B_xt[0]])
                    rmsnorm_block(P, None, xt[0][:, mb, :], gM, hb[mb][:], B_xt[0], B_hb[mb], (junk, ssb[mb], B_tmp[mb]))
                    transpose_block(P, hb[mb], B_hb[mb], ptr, B_ptr, memT[:, :, mb * 128:(mb + 1) * 128], B_memT, eng="act")
                for c in range(8):
                    pb, B_pb = prot.next()
                    for kc in range(8):
                        P.op("pe", lambda e, pb=pb, kc=kc, c=c: e.matmul(pb[:, 0:256], lhsT=wkv[:, kc, c * 128:(c + 1) * 128], rhs=memT[:, kc, :],
                                                                        start=(kc == 0), stop=(kc == 7)),
                             reads=[B_wkvq[c // 4], B_memT], writes=[B_pb], sig=(kc == 7))
                    P.op("dve", lambda e, pb=pb, c=c: e.tensor_copy(out=kxT[:, c, :], in_=pb[:, 0:256]), reads=[B_pb], writes=[B_kx])
                for mb in range(2):
                    for half in range(2):
                        pb, B_pb = prot.next()
                        for kc in range(8):
                            P.op("pe", lambda e, pb=pb, kc=kc, mb=mb, half=half: e.matmul(
                                pb[:], lhsT=memT[:, kc, mb * 128:(mb + 1) * 128], rhs=wkv[:, kc, D + half * 512:D + (half + 1) * 512],
                                start=(kc == 0), stop=(kc == 7)),
                                reads=[B_wkvq[2 + half], B_memT], writes=[B_pb], sig=(kc == 7))
                        P.op("act", lambda e, pb=pb, mb=mb, half=half: e.copy(out=vx[:, mb, half * 512:(half + 1) * 512], in_=pb[:]),
                             reads=[B_pb], writes=[B_vx])
                def f_load(ti):
                    t = full_tiles[ti]
                    s = ti % 2
                    P.dma("sp", [(xt[s][:, j, :], xres[(t * 4 + j) * 128:(t * 4 + j + 1) * 128, :]) for j in range(4)], B_xt[s], writes=[B_xt[s]])

                def f_norm_ew(ti, blocks=range(4)):
                    s = ti % 2
                    for j in blocks:
                        rmsnorm_block(P, None, xt[s][:, j, :], gX, hb[j][:], B_xt[s], B_hb[j], (junk, ssb[j % 2], B_tmp[j % 2]))

                def f_norm_tr(ti):
                    s = ti % 2
                    for j in range(4):
                        transpose_block(P, hb[j], B_hb[j], ptr, B_ptr, hT2[s][:, :, j * 128:(j + 1) * 128], B_hT2[s],
                                        eng=("act" if j % 2 == 0 else "dve"))

                def f_main(ti):
                    t = full_tiles[ti]
                    s = ti % 2
                    hT, B_hT = hT2[s], B_hT2[s]
                    if ti + 1 < len(full_tiles):
                        f_load(ti + 1)
                    for c in range(8):
                        pb, B_pb = prot.next()
                        for kc in range(8):
                            P.op("pe", lambda e, pb=pb, kc=kc, c=c: e.matmul(pb[:], lhsT=wq[:, kc, c * 128:(c + 1) * 128], rhs=hT[:, kc, :],
                                                                            start=(kc == 0), stop=(kc == 7)),
                                 reads=[B_wqq[c // 4], B_hT], writes=[B_pb], sig=(kc == 7))
                        if c % 2 == 0:
                            P.op("act", lambda e, pb=pb, c=c: e.copy(out=qxT[:, c, :], in_=pb[:]), reads=[B_pb], writes=[B_qx])
                        else:
                            P.op("dve", lambda e, pb=pb, c=c: e.tensor_copy(out=qxT[:, c, :], in_=pb[:]), reads=[B_pb], writes=[B_qx])
                    def f_s(hd):
                        for mb in range(2):
                            pb, B_pb = prot.next()
                            for dc in range(2):
                                P.op("pe", lambda e: e.matmul(
                                    pb[:], lhsT=kxT[:, hd * 2 + dc, mb * 128:(mb + 1) * 128], rhs=qxT[:, hd * 2 + dc, :], start=(dc == 0), stop=(dc == 1)),
                                    reads=[B_kx, B_qx], writes=[B_pb], sig=(dc == 1))
                            pi = (hd % 2) * 2 + mb
                            P.op("act", lambda e: e.activation(out=pTs[pi][:], in_=pb[:], func=AF.Exp, scale=1.0 / 16.0),
                                 reads=[B_pb], writes=[B_pT[pi]])

                    def f_pv(hd):
                        par = hd % 2
                        dpb, B_dpb = prot.next()
                        for mb in range(2):
                            pi = par * 2 + mb
                            P.op("pe", lambda e: e.matmul(dpb[:], lhsT=ones_f_b, rhs=pTs[pi][:], start=(mb == 0), stop=(mb == 1)),
                                 reads=[B_pT[pi], B_const], writes=[B_dpb], sig=(mb == 1))
                        for dc in range(2):
                            yp, B_yp = yps[par * 2 + dc], B_yps[par * 2 + dc]
                            for mb in range(2):
                                pi = par * 2 + mb
                                P.op("pe", lambda e: e.matmul(
                                    yp[:], lhsT=vx[:, mb, hd * 256 + dc * 128:hd * 256 + (dc + 1) * 128], rhs=pTs[pi][:], start=(mb == 0), stop=(mb == 1)),
                                    reads=[B_pT[pi], B_vx], writes=[B_yp], sig=(mb == 1))
                        rb_, B_rb_ = rbs2[par], B_rbs2[par]
                        P.op("act", lambda e: e.activation(out=rb_[:], in_=dpb[:], func=AF.Ln), reads=[B_dpb], writes=[B_rb_])
                        P.op("act", lambda e: e.activation(out=rb_[:], in_=rb_[:], func=AF.Exp, scale=-1.0), reads=[B_rb_], writes=[B_rb_])
                        for dc in range(2):
                            yp, B_yp = yps[par * 2 + dc], B_yps[par * 2 + dc]
                            P.op("dve", lambda e: e.tensor_tensor(out=yxT[:, hd * 2 + dc, :], in0=yp[:], in1=rb_[:], op=ALU.mult),
                                 reads=[B_yp, B_rb_], writes=[B_yx])

                    f_s(0)
                    for hd in range(4):
                        if hd + 1 < 4:
                            f_s(hd + 1)
                        f_pv(hd)

                def f_main_b(ti):
                    t = full_tiles[ti]
                    s = ti % 2
                    for j in range(4):
                        blk = t * 4 + j
                        for half in range(2):
                            po, B_po = prot.next()
                            for kc in range(8):
                                P.op("pe", lambda e, po=po, kc=kc, j=j, half=half: e.matmul(
                                    po[:], lhsT=yxT[:, kc, j * 128:(j + 1) * 128], rhs=wo[:, kc, half * 512:(half + 1) * 512], start=(kc == 0), stop=(kc == 7)),
                                    reads=[B_yx, B_wo], writes=[B_po], sig=(kc == 7))
                            P.op("dve", lambda e, po=po, s=s, j=j, half=half: e.tensor_tensor(
                                out=xt[s][:, j, half * 512:(half + 1) * 512], in0=po[:], in1=xt[s][:, j, half * 512:(half + 1) * 512], op=ALU.add),
                                reads=[B_po, B_xt[s]], writes=[B_xt[s]])
                        if ti + 1 < len(full_tiles):
                            f_norm_ew(ti + 1, blocks=[j])
                    P.dma("pool", [(xres[t * 512:(t + 1) * 512, :].rearrange("(j p) d -> p j d", p=128), xt[s][:])], B_xt[s], reads=[B_xt[s]])

                f_load(0)
                f_norm_ew(0)
                f_norm_tr(0)
                for ti in range(len(full_tiles)):
                    f_main(ti)
                    f_main_b(ti)
                    if ti + 1 < len(full_tiles):
                        f_norm_tr(ti + 1)
                P.run_phase("F%d" % l)
            if stop_after == ("F", l):
                break

            with ExitStack() as st:
                wg = sb("G_wg%d" % l, [128, 8, DFF], BF16, st)
                wu = sb("G_wu%d" % l, [128, 8, DFF], BF16, st)
                gcols = [(0, 768), (768, 1536), (1536, 2176), (2176, DFF)]

                def gq(c):
                    return 0 if c < 6 else (1 if c < 12 else (2 if c < 17 else 3))
                B_wgq = [Buf() for _ in range(4)]
                B_wuq = [Buf() for _ in range(4)]
                for q, (c0_, c1_) in enumerate(gcols):
                    P.dma("pool", [(wg[:, k, c0_:c1_], w_fg[l, k * 128:(k + 1) * 128, c0_:c1_]) for k in range(8)], B_wgq[q], writes=[B_wgq[q]])
                    P.dma("pool", [(wu[:, k, c0_:c1_], w_fu[l, k * 128:(k + 1) * 128, c0_:c1_]) for k in range(8)], B_wuq[q], writes=[B_wuq[q]])
                gF = gain_tile(st, 6 + l)
                xb = [sb("G_x%d_%d" % (l, i), [128, D], F32, st) for i in range(4)]
                B_xb = [Buf() for _ in range(4)]
                junk = sb("G_junk%d" % l, [128, D], BF16, st)
                ssb = [sb("G_ss%d_%d" % (l, i), [128, 4], F32, st) for i in range(2)]
                B_tmp = [Buf() for _ in range(2)]
                hb = [sb("G_h%d_%d" % (l, i), [128, D], BF16, st) for i in range(4)]
                B_hb = [Buf() for _ in range(4)]
                hT = [sb("G_hT%d_%d" % (l, i), [128, 8, 512], BF16, st) for i in range(2)]
                B_hT = [Buf() for _ in range(2)]
                aT = [sb("G_aT%d_%d" % (l, i), [128, NFF, 512], BF16, st) for i in range(2)]
                B_aT = [Buf() for _ in range(2)]
                sil = [sb("G_sil%d_%d" % (l, i), [128, 512], F32, st) for i in range(3)]
                silrot = Rot([(sil[i], Buf()) for i in range(3)])
                ptr2 = [ps("G_pt%d_%d" % (l, i), [128, 8, 128], BF16, st) for i in range(1)] * 2
                B_ptr2 = [Buf()] * 2
                pbanks = [ps("G_ps%d_%d" % (l, i), [128, 512], F32, st) for i in range(7)]
                prot = Rot([(pbanks[i], Buf()) for i in range(7)])
                def g_norm_ew(ti):
                    t = full_tiles[ti]
                    for j in range(4):
                        blk = t * 4 + j
                        P.dma("sp", [(xb[j][:], xres[blk * 128:(blk + 1) * 128, :])], B_xb[j], writes=[B_xb[j]])
                        rmsnorm_block(P, None, xb[j][:], gF, hb[j][:], B_xb[j], B_hb[j], (junk, ssb[j % 2], B_tmp[j % 2]))

                def g_norm_tr(ti):
                    s = ti % 2
                    for j in range(4):
                        transpose_block(P, hb[j], B_hb[j], ptr2[j % 2], B_ptr2[j % 2], hT[s][:, :, j * 128:(j + 1) * 128], B_hT[s],
                                        eng=("act" if j % 2 == 0 else "dve"))

                def g_main(ti):
                    t = full_tiles[ti]
                    s = ti % 2
                    for c in range(NFF):
                        pg, B_pg = prot.next()
                        for kc in range(8):
                            P.op("pe", lambda e, pg=pg, kc=kc, c=c, s=s: e.matmul(pg[:], lhsT=wg[:, kc, c * 128:(c + 1) * 128], rhs=hT[s][:, kc, :],
                                                                                 start=(kc == 0), stop=(kc == 7)),
                                 reads=[B_wgq[gq(c)], B_hT[s]], writes=[B_pg], sig=(kc == 7))
                        pu, B_pu = prot.next()
                        for kc in range(8):
                            P.op("pe", lambda e, pu=pu, kc=kc, c=c, s=s: e.matmul(pu[:], lhsT=wu[:, kc, c * 128:(c + 1) * 128], rhs=hT[s][:, kc, :],
                                                                                 start=(kc == 0), stop=(kc == 7)),
                                 reads=[B_wuq[gq(c)], B_hT[s]], writes=[B_pu], sig=(kc == 7))
                        sl, B_sl = silrot.next()
                        P.op("act", lambda e, sl=sl, pg=pg: e.activation(out=sl[:], in_=pg[:], func=AF.Silu), reads=[B_pg], writes=[B_sl])
                        P.op("dve", lambda e, sl=sl, pu=pu, c=c, s=s: e.tensor_tensor(out=aT[s][:, c, :], in0=pu[:], in1=sl[:], op=ALU.mult),
                             reads=[B_pu, B_sl], writes=[B_aT[s]])
                    P.dma("sp", [(aT_d[:, :, t * 512:(t + 1) * 512].rearrange("c p t -> p c t"), aT[s][:])], B_aT[s], reads=[B_aT[s]])

                g_norm_ew(0)
                g_norm_tr(0)
                for ti in range(len(full_tiles)):
                    if ti + 1 < len(full_tiles):
                        g_norm_ew(ti + 1)
                    g_main(ti)
                    if ti + 1 < len(full_tiles):
                        g_norm_tr(ti + 1)
                P.run_phase("G1_%d" % l)

            with ExitStack() as st:
                last = (l == n_layers - 1)
                wd = sb("H_wd%d" % l, [128, NFF, D], BF16, st)
                B_wdh = [Buf() for _ in range(2)]
                for hh in range(2):
                    P.dma("pool", [(wd[:, k, hh * 512:(hh + 1) * 512], w_fd[l, k * 128:(k + 1) * 128, hh * 512:(hh + 1) * 512]) for k in range(NFF)],
                          B_wdh[hh], writes=[B_wdh[hh]])
                if last:
                    gL = gain_tile(st, 8)
                    junk = sb("H_junk%d" % l, [128, D], BF16, st)
                    ssb = [sb("H_ss%d_%d" % (l, i), [128, 4], F32, st) for i in range(2)]
                    B_tmp = [Buf() for _ in range(2)]
                    ob = [sb("H_o%d_%d" % (l, i), [128, D], F32, st) for i in range(2)]
                    B_ob = [Buf() for _ in range(2)]
                aT = [sb("H_aT%d_%d" % (l, i), [128, NFF, 512], BF16, st) for i in range(2)]
                B_aT = [Buf() for _ in range(2)]
                xb = [sb("H_x%d_%d" % (l, i), [128, D], F32, st) for i in range(2)]
                B_xb = [Buf() for _ in range(2)]
                pbanks = [ps("H_ps%d_%d" % (l, i), [128, 512], F32, st) for i in range(8)]
                prot = Rot([(pbanks[i], Buf()) for i in range(8)])
                def h_load(ti):
                    t = full_tiles[ti]
                    s = ti % 2
                    P.dma("sp", [(aT[s][:], aT_d[:, :, t * 512:(t + 1) * 512].rearrange("c p t -> p c t"))], B_aT[s], writes=[B_aT[s]])

                h_load(0)
                for ti, t in enumerate(full_tiles):
                    s = ti % 2
                    if ti + 1 < len(full_tiles):
                        h_load(ti + 1)
                    for j in range(4):
                        blk = t * 4 + j
                        xs = blk % 2
                        P.dma("sp", [(xb[xs][:], xres[blk * 128:(blk + 1) * 128, :])], B_xb[xs], writes=[B_xb[xs]])
                        for half in range(2):
                            po, B_po = prot.next()
                            for c in range(NFF):
                                P.op("pe", lambda e, po=po, c=c, j=j, half=half, s=s: e.matmul(
                                    po[:], lhsT=aT[s][:, c, j * 128:(j + 1) * 128], rhs=wd[:, c, half * 512:(half + 1) * 512], start=(c == 0), stop=(c == NFF - 1)),
                                    reads=[B_aT[s], B_wdh[half]], writes=[B_po], sig=(c == NFF - 1))
                            P.op("dve", lambda e, po=po, xs=xs, half=half: e.tensor_tensor(
                                out=xb[xs][:, half * 512:(half + 1) * 512], in0=po[:], in1=xb[xs][:, half * 512:(half + 1) * 512], op=ALU.add),
                                reads=[B_po, B_xb[xs]], writes=[B_xb[xs]])
                        if last:
                            rmsnorm_block(P, None, xb[xs][:], gL, ob[xs][:], B_xb[xs], B_ob[xs], (junk, ssb[xs], B_tmp[xs]))
                            orow = (blk - NPREV_T * 4) * 128
                            P.dma("pool", [(out_d[orow:orow + 128, :], ob[xs][:])], B_ob[xs], reads=[B_ob[xs]])
                        else:
                            P.dma("pool", [(xres[blk * 128:(blk + 1) * 128, :], xb[xs][:])], B_xb[xs], reads=[B_xb[xs]])
                P.run_phase("G2_%d" % l)
            if stop_after == ("G", l):
                break
    return nc


def _t5_bucket_np(n):
    n = np.maximum(n, 0)
    max_exact = 16
    nf = np.maximum(n, 1).astype(np.float32)
    large = max_exact + (np.log(nf / np.float32(max_exact)) / np.float32(np.log(128 / max_exact)) * np.float32(32 - max_exact)).astype(np.int32)
    large = np.minimum(large, 31)
    return np.where(n < max_exact, n, large)


def _consts():
    cf = np.zeros((128, 128 * 3 + 512), np.float32)
    j = np.arange(128)[:, None]
    t = np.arange(128)[None, :]
    cf[:, 0:128] = (j <= t)
    cf[:, 128:256] = 1.0
    cf[0, 256:384] = 1.0
    cf[64, 384:448] = 1.0
    cb = np.zeros((128, 128 * 2 + 16 + 128), np.float32)
    cb[:, 0:128] = np.eye(128)
    cb[:, 128:256] = (j <= t)
    cb[:, 272:400] = 1.0
    return cf, cb.astype(ml_dtypes.bfloat16)


def _swa_bias_table(rel_bias):
    k = np.arange(128)[:, None]
    q = np.arange(128)[None, :]
    tab = np.full((128, 8, 256), NEG, np.float32)
    for part, dist in ((0, q - k), (1, 128 + q - k)):
        ok = (dist >= 0) & (dist < 128)
        bk = _t5_bucket_np(dist)
        for h in range(8):
            vals = rel_bias[bk, h]
            tab[:, h, part * 128:(part + 1) * 128] = np.where(ok, vals, np.float32(NEG))
    return tab


_NC_CACHE = {}


def _get_nc(**kw):
    key = tuple(sorted((k, str(v)) for k, v in kw.items()))
    if key not in _NC_CACHE:
        _NC_CACHE[key] = build_program(**kw)
    return _NC_CACHE[key]


def make_in_maps(inputs, cores=range(8)):
    f32 = lambda a: np.ascontiguousarray(np.asarray(a, dtype=np.float32))
    x = f32(inputs["x"])
    mem = f32(inputs["mem"])
    cf, cb = _consts()
    swab = _swa_bias_table(f32(inputs["rel_bias"]))
    gains = np.concatenate([f32(inputs["mix_norm_g"]), f32(inputs["xattn_norm_g"]), f32(inputs["mem_norm_g"]),
                            f32(inputs["ffn_norm_g"]), f32(inputs["final_norm_g"])[None, :]], axis=0)
    conv_wT = np.ascontiguousarray(np.transpose(f32(inputs["conv_w"]), (0, 2, 1)))
    shared = {
        "cf": cf, "cb": cb, "swab": swab, "gains": gains, "conv_wT": conv_wT,
        "w_in": f32(inputs["w_in"]), "w_branch": f32(inputs["w_branch"]), "w_mix_out": f32(inputs["w_mix_out"]),
        "w_xq": f32(inputs["w_xq"]), "w_xkv": f32(inputs["w_xkv"]), "w_xo": f32(inputs["w_xo"]),
        "w_ffn_gate": f32(inputs["w_ffn_gate"]), "w_ffn_up": f32(inputs["w_ffn_up"]), "w_ffn_down": f32(inputs["w_ffn_down"]),
        "forget_bias": f32(inputs["forget_bias"]), "sink": f32(inputs["sink"]),
    }
    maps = []
    for c in cores:
        b, half = c // 2, c % 2
        if half == 0:
            xc = np.concatenate([np.zeros((T // 2, D), np.float32), x[b, :T // 2]], axis=0)
            valid = np.zeros((128, NB), np.float32)
            valid[:, NB // 2:] = 1.0
        else:
            xc = x[b]
            valid = np.ones((128, NB), np.float32)
        m = dict(shared)
        m["x"] = np.ascontiguousarray(xc)
        m["mem"] = np.ascontiguousarray(mem[b])
        m["valid"] = valid
        maps.append(m)
    return maps


def kernel(**inputs):
    nc = _get_nc()
    maps = make_in_maps(inputs)
    res = run_bass_kernel_spmd(nc, maps, core_ids=list(range(8)))
    out = np.zeros((4, T, D), np.float32)
    for c in range(8):
        b, half = c // 2, c % 2
        out[b, half * (T // 2):(half + 1) * (T // 2)] = res.results[c]["out"]
    return out
```

```python
import numpy as np
import ml_dtypes
from contextlib import ExitStack
import concourse.bass as bass
import concourse.mybir as mybir
from concourse.bass_utils import run_bass_kernel_spmd

F32 = mybir.dt.float32
BF16 = mybir.dt.bfloat16
AF = mybir.ActivationFunctionType
ALU = mybir.AluOpType

D = 1024
T = 8192
NB = 64
NT = 16
NPREV_T = 8
DFF = 2816
NFF = 22
INC = 6920
EPS = 1e-6
NEG = -30000.0
DEBUG_TILES = 0
DBG = set()

C_B, C_C, C_U, C_FQ, C_FK, C_FV, C_FG, C_SQ, C_SK, C_SV, C_G = 0, 512, 1024, 1536, 2048, 2560, 3072, 3080, 3592, 3720, 3848

ENGS = ("pe", "act", "dve", "pool", "sp")


class Buf:
    __slots__ = ("w", "r", "sem", "name")

    def __init__(self, name=""):
        self.w = None
        self.r = []
        self.sem = None
        self.name = name


class _Rec:
    def __getattr__(self, name):
        def f(*a, **k):
            return (name, a, k)
        return f


_REC = _Rec()


class Prog:
    def __init__(self, nc, esets, dhw_sets, dsw):
        self.nc = nc
        self.esets = esets
        self.dhw_sets = dhw_sets
        self.dsw = list(dsw)
        self.swids = set(id(s) for s in self.dsw)
        self.to_clear = []
        self.reset()

    def reset(self):
        self.phase = getattr(self, "phase", 0) + 1
        cur = self.phase % 2
        self.esem = dict(zip(ENGS, self.esets[cur]))
        self.ops = {e: [] for e in ENGS}
        self.ecount = {e: 0 for e in ENGS}
        self.dcount = {s: v for s, v in getattr(self, "dcount", {}).items() if id(s) in self.swids}
        self.dfree = {"hw": list(self.dhw_sets[cur]), "sw": list(self.dsw)}
        self.known = {e: {} for e in ENGS}

    def _dsem(self, buf, eng):
        kind = "sw" if eng == "pool" else "hw"
        if buf.sem is None or buf.sem[0] != self.phase:
            buf.sem = (self.phase, {})
        d = buf.sem[1]
        if kind not in d:
            s = self.dfree[kind].pop()
            d[kind] = s
            if kind == "hw" or s not in self.dcount:
                self.dcount[s] = 0
        return d[kind]

    def _waits(self, eng, reads, writes):
        w = {}

        def add(ev):
            if ev is None:
                return
            s, v, src, ph = ev
            if ph != self.phase:
                return
            if src == eng and eng == "pe":
                return
            if self.known[eng].get(s, 0) >= v:
                return
            if w.get(s, (0,))[0] < v:
                w[s] = (v,)

        for b in reads:
            add(b.w)
        for b in writes:
            add(b.w)
            for ev in b.r:
                add(ev)
        out = []
        for s, (v,) in w.items():
            self.known[eng][s] = v
            out.append((s, v))
        return out

    def op(self, eng, fn, reads=(), writes=(), sig=True):
        name_, a_, k_ = fn(_REC)
        fn = (lambda e, name_=name_, a_=a_, k_=k_: getattr(e, name_)(*a_, **k_))
        waits = self._waits(eng, reads, writes)
        if sig:
            self.ecount[eng] += 1
            ev = (self.esem[eng], self.ecount[eng], eng, self.phase)
        else:
            ev = None
        self.ops[eng].append((waits, fn, (self.esem[eng], 1) if sig else None))
        if ev is not None:
            for b in reads:
                b.r.append(ev)
            for b in writes:
                b.w = ev
                b.r = []
        return ev

    def dma(self, eng, pairs, sbuf, reads=(), writes=()):
        waits = self._waits(eng, reads, writes)
        s = self._dsem(sbuf, eng)
        self.dcount[s] += 16 * len(pairs)
        ev = (s, self.dcount[s], "dma", self.phase)

        def fn(e, pairs=pairs):
            return [e.dma_start(out=o, in_=i) for (o, i) in pairs]

        self.ops[eng].append((waits, fn, (s, 16)))
        for b in reads:
            b.r.append(ev)
        for b in writes:
            b.w = ev
            b.r = []
        return ev

    def run_phase(self, name):
        nc = self.nc
        final_waits = [(s, v) for s, v in self.dcount.items() if v > 0]
        used = [s for s, v in self.dcount.items() if v > 0 and id(s) not in self.swids] + [self.esem[e] for e in ENGS if self.ecount[e] > 0]
        ops = self.ops
        to_clear = self.to_clear
        with nc.Block() as block:
            def mk(ename):
                def body(eng):
                    for waits, fn, inc in ops[ename]:
                        for s, v in waits:
                            eng.wait_ge(s, v)
                        ins = fn(eng)
                        if inc is not None:
                            if isinstance(ins, list):
                                for i in ins:
                                    i.then_inc(inc[0], inc[1])
                            else:
                                ins.then_inc(inc[0], inc[1])
                    if ename == "pool":
                        for s, v in final_waits:
                            eng.wait_ge(s, v)
                        for s in to_clear:
                            eng.sem_clear(s)
                return body
            block.tensor(mk("pe"))
            block.scalar(mk("act"))
            block.vector(mk("dve"))
            block.gpsimd(mk("pool"))
            block.sync(mk("sp"))
        self.to_clear = used
        self.reset()


def build_program(debug_outs=(), n_layers=2, stop_after=None):
    nc = bass.Bass("TRN2", target_bir_lowering=False)
    dbg = set(debug_outs)

    def din(name, shape, dt=F32):
        return nc.dram_tensor(name, list(shape), dt, kind="ExternalInput").ap()

    def dscr(name, shape, dt):
        kind = "ExternalOutput" if name in dbg else "Internal"
        return nc.dram_tensor(name, list(shape), dt, kind=kind).ap()

    x_in = din("x", [T, D])
    mem_in = din("mem", [256, D])
    valid_in = din("valid", [128, NB])
    cf_in = din("cf", [128, 128 * 3 + 4 * 128])
    cb_in = din("cb", [128, 128 * 3 + 16], BF16)
    swab_in = din("swab", [128, 8, 256])
    w_in = din("w_in", [2, D, INC])
    w_branch = din("w_branch", [2, 3, 512, D])
    w_mix_out = din("w_mix_out", [2, D, D])
    w_xq = din("w_xq", [2, D, D])
    w_xkv = din("w_xkv", [2, D, 2 * D])
    w_xo = din("w_xo", [2, D, D])
    w_fg = din("w_ffn_gate", [2, D, DFF])
    w_fu = din("w_ffn_up", [2, D, DFF])
    w_fd = din("w_ffn_down", [2, DFF, D])
    gains = din("gains", [9, D])
    fbias = din("forget_bias", [2, 8])
    sink_in = din("sink", [2, 8])
    convw = din("conv_wT", [2, 512, 3])
    out_d = nc.dram_tensor("out", [T // 2, D], F32, kind="ExternalOutput").ap()

    xres = dscr("xres", [T, D], F32)
    hT_d = dscr("hT_d", [8, 128, T], BF16)
    ycv_d = dscr("ycv_d", [4, 128, T], BF16)
    yfx_d = dscr("yfx_d", [4, 128, T], BF16)
    ysw_d = dscr("ysw_d", [4, 128, T], BF16)
    fq_d = dscr("fq_d", [4, 128, T], BF16)
    fk_d = dscr("fk_d", [4, 128, T], BF16)
    sq_d = dscr("sq_d", [4, 128, T], BF16)
    sk_d = dscr("sk_d", [128, T], BF16)
    fv_d = dscr("fv_d", [8, 128, NB, 65], BF16)
    sv_d = dscr("sv_d", [2, 128, NB, 65], BF16)
    negc_d = dscr("negc_d", [128, NB * 8], F32)
    aT_d = dscr("aT_d", [NFF, 128, T], BF16)

    with ExitStack() as top:
        def sb(name, shape, dt, st=top):
            return st.enter_context(nc.sbuf_tensor(name, list(shape), dt))

        def ps(name, shape, dt, st):
            return st.enter_context(nc.psum_tensor(name, list(shape), dt))

        esets = [[top.enter_context(nc.semaphore("e%d_%s" % (k, e))) for e in ENGS] for k in range(2)]
        dhw_sets = [[top.enter_context(nc.semaphore("h%d_%d" % (k, i))) for i in range(30)] for k in range(2)]
        dsw = [top.enter_context(nc.semaphore("s%d" % i)) for i in range(24)]
        P = Prog(nc, esets, dhw_sets, dsw)

        cf = sb("cf_sb", [128, 128 * 3 + 4 * 128], F32)
        cb = sb("cb_sb", [128, 128 * 3 + 16], BF16)
        valid = sb("valid_sb", [128, NB], F32)
        fb_sb = sb("fb_sb", [128, 2, 8], F32)
        esink = sb("esink_sb", [128, 2, 8], F32)
        cw_sb = sb("cw_sb", [128, 2, 4, 3], F32)
        flog = sb("flog_sb", [128, NB, 8], F32)
        negc = sb("negc_sb", [128, NB, 8], F32)
        negc0 = sb("negc0_sb", [128, NB, 8], F32)
        B_const = Buf("const")
        B_flog = Buf("flog")
        B_negc = Buf("negc")

        U_f = cf[:, 0:128]
        ones_f = cf[:, 128:256]
        sel0_f = cf[:, 256:384]
        sel64_f = cf[:, 384:448]
        ident_b = cb[:, 0:128]
        tri_b = cb[:, 128:256]
        ones_f_b = cb[:, 272:400]

        P.dma("sp", [(cf[:], cf_in[:, :]), (cb[:], cb_in[:, :]), (valid[:], valid_in[:, :])], B_const, writes=[B_const])
        P.dma("sp", [(fb_sb[:, i, :], fbias[i:i + 1, :].partition_broadcast(128)) for i in range(2)]
              + [(esink[:, i, :], sink_in[i:i + 1, :].partition_broadcast(128)) for i in range(2)]
              + [(cw_sb[:, i, :, :], convw[i].rearrange("(c p) k -> p c k", p=128)) for i in range(2)],
              B_const, writes=[B_const])
        P.op("act", lambda e: e.activation(out=esink[:], in_=esink[:], func=AF.Exp), reads=[B_const], writes=[B_const])
        P.op("pool", lambda e: e.memset(flog[:], 0.0), writes=[B_flog])
        P.run_phase("const")
        if stop_after == ("0",):
            return nc

        gcount = [0]

        def gain_tile(st, gi):
            gcount[0] += 1
            gt = sb("gain%d" % gcount[0], [128, D], F32, st)
            B_g = Buf()
            P.dma("sp", [(gt[:], gains[gi:gi + 1, :].partition_broadcast(128))], B_g, writes=[B_g])
            return gt, B_g

        def rmsnorm_block(P, st_bufs, x_ap, gi, h_out, B_x, B_h, tmp):
            junk, ss, B_tmp = tmp
            P.op("act", lambda e: e.activation(out=junk[:], in_=x_ap, func=AF.Square, accum_out=ss[:, 0:1]),
                 reads=[B_x], writes=[B_tmp])
            P.op("act", lambda e: e.activation(out=ss[:, 1:2], in_=ss[:, 0:1], func=AF.Sqrt, bias=EPS, scale=1.0 / D),
                 reads=[B_tmp], writes=[B_tmp])
            P.op("dve", lambda e: e.reciprocal(out=ss[:, 2:3], in_=ss[:, 1:2]), reads=[B_tmp], writes=[B_tmp])
            P.op("dve", lambda e: e.scalar_tensor_tensor(out=h_out, in0=x_ap, scalar=ss[:, 2:3], in1=gi[0][:],
                                                         op0=ALU.mult, op1=ALU.mult),
                 reads=[B_x, B_tmp, gi[1]], writes=[B_h])

        def transpose_block(P, h_ap, B_h, pt, B_pt, hT_dst, B_hT, eng="act"):
            for kc in range(8):
                P.op("pe", lambda e, kc=kc: e.transpose(out=pt[:, kc, :], in_=h_ap[:, kc * 128:(kc + 1) * 128], identity=ident_b),
                     reads=[B_h, B_const], writes=[B_pt], sig=(kc == 7))
            if eng == "act":
                P.op("act", lambda e: e.copy(out=hT_dst, in_=pt[:]), reads=[B_pt], writes=[B_hT])
            else:
                P.op("dve", lambda e: e.tensor_copy(out=hT_dst, in_=pt[:]), reads=[B_pt], writes=[B_hT])

        def load_w(P, dst, B_dst, src2d, ncols, nk=8, rows_per=128):
            pairs = []
            for k in range(nk):
                pairs.append((dst[:, k, :], src2d[k * 128:(k + 1) * 128, :]))
            P.dma("pool", pairs, B_dst, writes=[B_dst])

        class Rot:
            def __init__(self, items):
                self.items = items
                self.i = 0

            def next(self):
                it = self.items[self.i % len(self.items)]
                self.i += 1
                return it

        for l in range(n_layers):
            x_src = x_in if l == 0 else xres
            full_tiles = list(range(NT)) if l == 0 else list(range(NPREV_T, NT))
            a_tiles = list(range(NT))
            a_full = (lambda t: True) if l == 0 else (lambda t: t >= NPREV_T - 1)

            with ExitStack() as st:
                wfm = sb("A_wfm%d" % l, [128, 8, 3200], BF16, st)
                wtk = sb("A_wtk%d" % l, [128, 8, 648], BF16, st)
                B_wfm = [Buf() for _ in range(4)]
                B_wtk = Buf()
                wl = w_in[l]
                P.dma("pool", [(wtk[:, k, 0:512], wl[k * 128:(k + 1) * 128, C_FV:C_FV + 512]) for k in range(8)]
                      + [(wtk[:, k, 512:520], wl[k * 128:(k + 1) * 128, C_FG:C_FG + 8]) for k in range(8)]
                      + [(wtk[:, k, 520:648], wl[k * 128:(k + 1) * 128, C_SV:C_SV + 128]) for k in range(8)],
                      B_wtk, writes=[B_wtk])
                segs = [(0, 0, 1536), (1536, 1536, 1024), (2560, C_SQ, 640)]
                for si, (d0, s0, n) in sorted(enumerate(segs), key=lambda q: (q[0] + 2) % 3):
                    P.dma("pool", [(wfm[:, k, d0:d0 + n], wl[k * 128:(k + 1) * 128, s0:s0 + n]) for k in range(8)],
                          B_wfm[si], writes=[B_wfm[si]])

                def wfm_buf(col):
                    return B_wfm[0] if col < 1536 else (B_wfm[1] if col < 2560 else B_wfm[2])

                gA = gain_tile(st, 0 + l)
                xb = [sb("A_x%d_%d" % (l, i), [128, D], F32, st) for i in range(4)]
                B_xb = [Buf() for _ in range(4)]
                junk = sb("A_junk%d" % l, [128, D], BF16, st)
                ssb = [sb("A_ss%d_%d" % (l, i), [128, 4], F32, st) for i in range(2)]
                B_tmp = [Buf() for _ in range(2)]
                hb = [sb("A_h%d_%d" % (l, i), [128, D], BF16, st) for i in range(4)]
                B_hb = [Buf() for _ in range(4)]
                hT = [sb("A_hT%d_%d" % (l, i), [128, 8, 512], BF16, st) for i in range(2)]
                B_hT = [Buf() for _ in range(2)]
                ptr2 = [ps("A_pt%d_%d" % (l, i), [128, 8, 128], BF16, st) for i in range(1)] * 2
                B_ptr2 = [Buf()] * 2
                pbanks = [ps("A_ps%d_%d" % (l, i), [128, 512], F32, st) for i in range(7)]
                prot = Rot([(pbanks[i], Buf()) for i in range(7)])
                zc = [sb("A_z%d_%d" % (l, c), [128, 514], F32, st) for c in range(4)]
                B_zc = [Buf() for _ in range(4)]
                usb2 = [sb("A_u%d_%d" % (l, i), [128, 512], F32, st) for i in range(2)]
                B_usb2 = [Buf() for _ in range(2)]
                acc4 = [sb("A_acc%d_%d" % (l, i), [128, 512], F32, st) for i in range(4)]
                B_acc4 = [Buf() for _ in range(4)]
                ycv = [sb("A_ycv%d_%d" % (l, i), [128, 4, 512], BF16, st) for i in range(2)]
                B_ycv = [Buf() for _ in range(2)]
                stg = [sb("A_stg%d_%d" % (l, i), [128, 4, 512], BF16, st) for i in range(3)]
                srot = Rot([(stg[i], Buf()) for i in range(3)])
                vt = [sb("A_vt%d_%d" % (l, i), [128, 8, 4, 65], BF16, st) for i in range(2)]
                B_vt = [Buf() for _ in range(2)]
                svt = [sb("A_svt%d_%d" % (l, i), [128, 2, 4, 65], BF16, st) for i in range(2)]
                B_svt = [Buf() for _ in range(2)]

                for c in range(4):
                    P.op("pool", lambda e, c=c: e.memset(zc[c][:], 0.0), writes=[B_zc[c]])

                evac_i = [0]

                def evac_copy(P, out_ap, in_ap, reads, writes):
                    evac_i[0] += 1
                    if evac_i[0] % 2 == 0:
                        P.op("act", lambda e: e.copy(out=out_ap, in_=in_ap), reads=reads, writes=writes)
                    else:
                        P.op("dve", lambda e: e.tensor_copy(out=out_ap, in_=in_ap), reads=reads, writes=writes)

                def fm_chunk(P, col, hTt, B_hTt):
                    pb, B_pb = prot.next()
                    for kc in range(8):
                        P.op("pe", lambda e, kc=kc, pb=pb: e.matmul(pb[:], lhsT=wfm[:, kc, col:col + 128], rhs=hTt[:, kc, :],
                                                                   start=(kc == 0), stop=(kc == 7)),
                             reads=[wfm_buf(col), B_hTt], writes=[B_pb], sig=(kc == 7))
                    return pb, B_pb

                def a_norm_ew(ti):
                    t = a_tiles[ti]
                    for j in range(4):
                        blk = t * 4 + j
                        P.dma("sp", [(xb[j][:], x_src[blk * 128:(blk + 1) * 128, :])], B_xb[j], writes=[B_xb[j]])
                        rmsnorm_block(P, None, xb[j][:], gA, hb[j][:], B_xb[j], B_hb[j], (junk, ssb[j % 2], B_tmp[j % 2]))

                def a_norm_tr(ti):
                    t = a_tiles[ti]
                    full = a_full(t)
                    s = ti % 2
                    hTt, B_hTt = hT[s], B_hT[s]
                    for j in range(4):
                        transpose_block(P, hb[j], B_hb[j], ptr2[j % 2], B_ptr2[j % 2], hTt[:, :, j * 128:(j + 1) * 128], B_hTt,
                                        eng=("act" if j % 2 == 0 else "dve"))
                    if full:
                        P.dma("sp", [(hT_d[:, :, t * 512:(t + 1) * 512].rearrange("k p t -> p k t"), hTt[:])], B_hTt, reads=[B_hTt])
                def a_main(ti):
                    t = a_tiles[ti]
                    full = a_full(t)
                    s = ti % 2
                    hTt, B_hTt = hT[s], B_hT[s]
                    vs = ti % 2
                    for j in range(4):
                        blk = t * 4 + j
                        pv, B_pv = prot.next()
                        for kc in range(8):
                            P.op("pe", lambda e, kc=kc, pv=pv, j=j: e.matmul(pv[:], lhsT=hTt[:, kc, j * 128:(j + 1) * 128], rhs=wtk[:, kc, 0:512],
                                                                            start=(kc == 0), stop=(kc == 7)),
                                 reads=[B_wtk, B_hTt], writes=[B_pv], sig=(kc == 7))
                        pg, B_pg = prot.next()
                        for kc in range(8):
                            P.op("pe", lambda e, kc=kc, pg=pg, j=j: e.matmul(pg[:, 0:136], lhsT=hTt[:, kc, j * 128:(j + 1) * 128], rhs=wtk[:, kc, 512:648],
                                                                            start=(kc == 0), stop=(kc == 7)),
                                 reads=[B_wtk, B_hTt], writes=[B_pg], sig=(kc == 7))
                        vcol = valid[:, blk:blk + 1]
                        P.op("dve", lambda e, pv=pv, j=j, vcol=vcol: e.tensor_scalar(
                            out=vt[vs][:, :, j, 0:64], in0=pv[:].rearrange("p (h d) -> p h d", h=8), scalar1=vcol, scalar2=None, op0=ALU.mult),
                            reads=[B_pv, B_const], writes=[B_vt[vs]])
                        P.op("pool", lambda e, j=j, vcol=vcol: e.tensor_copy(out=vt[vs][:, :, j, 64:65], in_=vcol.unsqueeze(1).broadcast_to([128, 8, 1])),
                             reads=[B_const], writes=[B_vt[vs]])
                        P.op("dve", lambda e, pg=pg, j=j, vcol=vcol: e.tensor_scalar(
                            out=svt[vs][:, :, j, 0:64], in0=pg[:, 8:136].rearrange("p (h d) -> p h d", h=2), scalar1=vcol, scalar2=None, op0=ALU.mult),
                            reads=[B_pg, B_const], writes=[B_svt[vs]])
                        P.op("pool", lambda e, j=j, vcol=vcol: e.tensor_copy(out=svt[vs][:, :, j, 64:65], in_=vcol.unsqueeze(1).broadcast_to([128, 2, 1])),
                             reads=[B_const], writes=[B_svt[vs]])
                        P.op("dve", lambda e, pg=pg, blk=blk: e.tensor_tensor(out=flog[:, blk, :], in0=pg[:, 0:8], in1=fb_sb[:, l, :], op=ALU.add),
                             reads=[B_pg, B_const], writes=[B_flog])
                    P.dma("pool", [(fv_d[:, :, t * 4:(t + 1) * 4, :].rearrange("h p b d -> p h b d"), vt[vs][:])], B_vt[vs], reads=[B_vt[vs]])
                    P.dma("pool", [(sv_d[:, :, t * 4:(t + 1) * 4, :].rearrange("h p b d -> p h b d"), svt[vs][:])], B_svt[vs], reads=[B_svt[vs]])
                    if ti + 1 < len(a_tiles):
                        a_norm_ew(ti + 1)
                    groups = [("fk", 2048, fk_d, 4)]
                    if full:
                        groups = [("fq", 1536, fq_d, 4), ("fk", 2048, fk_d, 4), ("sq", 2560, sq_d, 4)]
                    for (nm, c0, dst, nchunk) in groups:
                        sg, B_sg = srot.next()
                        for c in range(nchunk):
                            pb, B_pb = fm_chunk(P, c0 + c * 128, hTt, B_hTt)
                            evac_copy(P, sg[:, c, :], pb[:], [B_pb], [B_sg])
                        P.dma("sp", [(dst[:, :, t * 512:(t + 1) * 512].rearrange("c p t -> p c t"), sg[:])], B_sg, reads=[B_sg])
                    sg, B_sg = srot.next()
                    pb, B_pb = fm_chunk(P, 3072, hTt, B_hTt)
                    evac_copy(P, sg[:, 0, :], pb[:], [B_pb], [B_sg])
                    P.dma("sp", [(sk_d[:, t * 512:(t + 1) * 512], sg[:, 0, :])], B_sg, reads=[B_sg])
                    if full:
                        ys = ti % 2
                        for c in range(4):
                            pB, B_pB = fm_chunk(P, 0 + c * 128, hTt, B_hTt)
                            pC, B_pC = fm_chunk(P, 512 + c * 128, hTt, B_hTt)
                            pU, B_pU = fm_chunk(P, 1024 + c * 128, hTt, B_hTt)
                            z = zc[c]
                            P.op("pool", lambda e, z=z: e.tensor_copy(out=z[:, 0:2], in_=z[:, 512:514]), reads=[B_zc[c]], writes=[B_zc[c]])
                            if t == NPREV_T:
                                P.op("pool", lambda e, z=z: e.tensor_scalar(out=z[:, 0:2], in0=z[:, 0:2], scalar1=valid[:, NPREV_T * 4 - 1:NPREV_T * 4],
                                                                            scalar2=None, op0=ALU.mult),
                                     reads=[B_zc[c], B_const], writes=[B_zc[c]])
                            usb, B_usb = usb2[c % 2], B_usb2[c % 2]
                            a0, a1 = acc4[(c % 2) * 2], acc4[(c % 2) * 2 + 1]
                            B_a0, B_a1 = B_acc4[(c % 2) * 2], B_acc4[(c % 2) * 2 + 1]
                            P.op("act", lambda e: e.copy(out=usb[:], in_=pU[:]), reads=[B_pU], writes=[B_usb])
                            P.op("dve", lambda e: e.tensor_tensor(out=z[:, 2:514], in0=pC[:], in1=usb[:], op=ALU.mult),
                                 reads=[B_pC, B_usb], writes=[B_zc[c]])
                            P.op("act", lambda e: e.activation(out=a0[:], in_=z[:, 2:514], func=AF.Copy, scale=cw_sb[:, l, c, 2:3]),
                                 reads=[B_zc[c], B_const], writes=[B_a0])
                            P.op("dve", lambda e: e.scalar_tensor_tensor(out=a1[:], in0=z[:, 1:513], scalar=cw_sb[:, l, c, 1:2], in1=a0[:],
                                                                         op0=ALU.mult, op1=ALU.add),
                                 reads=[B_zc[c], B_const, B_a0], writes=[B_a1])
                            P.op("dve", lambda e: e.scalar_tensor_tensor(out=a0[:], in0=z[:, 0:512], scalar=cw_sb[:, l, c, 0:1], in1=a1[:],
                                                                         op0=ALU.mult, op1=ALU.add),
                                 reads=[B_zc[c], B_const, B_a1], writes=[B_a0])
                            P.op("dve", lambda e: e.tensor_tensor(out=ycv[ys][:, c, :], in0=pB[:], in1=a0[:], op=ALU.mult),
                                 reads=[B_pB, B_a0], writes=[B_ycv[ys]])
                        P.dma("sp", [(ycv_d[:, :, t * 512:(t + 1) * 512].rearrange("c p t -> p c t"), ycv[ys][:])], B_ycv[ys], reads=[B_ycv[ys]])

                a_norm_ew(0)
                a_norm_tr(0)
                for ti in range(len(a_tiles)):
                    a_main(ti)
                    if ti + 1 < len(a_tiles):
                        a_norm_tr(ti + 1)
                P.run_phase("A%d" % l)
            if stop_after == ("A", l):
                break

            with ExitStack() as st:
                lsb = sb("B_l%d" % l, [128, NB, 8], F32, st)
                rsb = sb("B_r%d" % l, [128, NB, 8], F32, st)
                B_l, B_r = Buf(), Buf()
                pc = ps("B_pc%d" % l, [128, 512], F32, st)
                pc0 = ps("B_pc0%d" % l, [128, 512], F32, st)
                B_pc, B_pc0 = Buf(), Buf()
                P.op("act", lambda e: e.activation(out=lsb[:], in_=flog[:], func=AF.Exp, scale=-1.0), reads=[B_flog], writes=[B_l])
                P.op("act", lambda e: e.activation(out=lsb[:], in_=lsb[:], func=AF.Ln, bias=1.0, scale=1.0), reads=[B_l], writes=[B_l])
                P.op("dve", lambda e: e.memset(rsb[:, 0, :], 0.0), writes=[B_r])
                for b in range(1, NB):
                    P.op("dve", lambda e, b=b: e.tensor_tensor(out=rsb[:, b, :], in0=rsb[:, b - 1, :], in1=lsb[:, b - 1, :], op=ALU.add),
                         reads=[B_l, B_r], writes=[B_r])
                P.op("pe", lambda e: e.matmul(pc[:], lhsT=U_f, rhs=lsb[:].rearrange("p b h -> p (b h)"), start=True, stop=False),
                     reads=[B_l, B_const], writes=[B_pc], sig=False)
                P.op("pe", lambda e: e.matmul(pc[:], lhsT=ones_f, rhs=rsb[:].rearrange("p b h -> p (b h)"), start=False, stop=True),
                     reads=[B_r, B_const], writes=[B_pc])
                P.op("dve", lambda e: e.tensor_copy(out=negc[:].rearrange("p b h -> p (b h)"), in_=pc[:]), reads=[B_pc], writes=[B_negc])
                P.op("pe", lambda e: e.matmul(pc0[:], lhsT=sel0_f, rhs=negc[:].rearrange("p b h -> p (b h)"), start=True, stop=True),
                     reads=[B_negc, B_const], writes=[B_pc0])
                P.op("dve", lambda e: e.tensor_copy(out=negc0[:].rearrange("p b h -> p (b h)"), in_=pc0[:]), reads=[B_pc0], writes=[B_negc])
                if "negc_d" in dbg:
                    P.dma("sp", [(negc_d[:, :], negc[:].rearrange("p b h -> p (b h)"))], B_negc, reads=[B_negc])
                P.run_phase("B%d" % l)
            if stop_after == ("B", l):
                break

            g_list = list(range(16)) if l == 0 else list(range(8, 16))
            if DEBUG_TILES:
                g_list = g_list[:DEBUG_TILES]
                full_tiles = full_tiles[:DEBUG_TILES]

            def attn_finish(P, ya, B_ya, rr, B_rr, rb, B_rb, rbs, B_rbs, out_ap, B_out, sink_ap=None):
                if sink_ap is not None:
                    P.op("dve", lambda e: e.tensor_scalar(out=rr[64:65, :], in0=ya[64:65, :], scalar1=sink_ap, scalar2=1e-30, op0=ALU.add, op1=ALU.max),
                         reads=[B_ya, B_const], writes=[B_rr])
                else:
                    P.op("dve", lambda e: e.tensor_scalar(out=rr[64:65, :], in0=ya[64:65, :], scalar1=1e-30, scalar2=None, op0=ALU.max),
                         reads=[B_ya], writes=[B_rr])
                P.op("dve", lambda e: e.reciprocal(out=rr[64:65, :], in_=rr[64:65, :]), reads=[B_rr], writes=[B_rr])
                P.op("pe", lambda e: e.matmul(rb[0:64, :], lhsT=ones_f[64:65, 0:64], rhs=rr[64:65, :], start=True, stop=True),
                     reads=[B_rr, B_const], writes=[B_rb])
                P.op("act", lambda e: e.copy(out=rbs[:], in_=rb[0:64, :]), reads=[B_rb], writes=[B_rbs])
                P.op("dve", lambda e: e.tensor_tensor(out=out_ap, in0=ya[0:64, :], in1=rbs[:], op=ALU.mult),
                     reads=[B_ya, B_rbs], writes=[B_out])

            with ExitStack() as st:
                kz = [sb("C_kz%d_%d" % (l, i), [128, T], BF16, st) for i in range(2)]
                B_kz = [Buf() for _ in range(2)]
                qp = [sb("C_qp%d_%d" % (l, i), [128, T], BF16, st) for i in range(2)]
                B_qp = [Buf() for _ in range(2)]
                vh = [sb("C_vh%d_%d" % (l, i), [128, NB, 65], BF16, st) for i in range(2)]
                B_vh = [Buf() for _ in range(2)]
                biasg = [sb("C_bg%d_%d" % (l, i), [128, NB], F32, st) for i in range(2)]
                B_bg = [Buf() for _ in range(2)]
                pts = [sb("C_pt%d_%d" % (l, i), [128, 512], BF16, st) for i in range(6)]
                ptrot = Rot([(pts[i], Buf()) for i in range(6)])
                sps = [ps("C_s%d_%d" % (l, i), [128, 512], F32, st) for i in range(5)]
                srot2 = Rot([(sps[i], Buf()) for i in range(5)])
                yac = [ps("C_y%d_%d" % (l, i), [128, 512], F32, st) for i in range(2)]
                B_yac = [Buf() for _ in range(2)]
                rb = ps("C_rb%d" % l, [128, 512], F32, st)
                B_rb = Buf()
                rr = [sb("C_rr%d_%d" % (l, i), [128, 512], F32, st) for i in range(2)]
                B_rr = [Buf() for _ in range(2)]
                rbs = sb("C_rbs%d" % l, [64, 512], F32, st)
                B_rbs = Buf()
                yo = [sb("C_yo%d_%d" % (l, i), [64, 512], BF16, st) for i in range(2)]
                B_yo = [Buf() for _ in range(2)]
                for i_ in range(2):
                    P.op("pool", lambda e: e.memset(rr[i_][:], 0.0), writes=[B_rr[i_]])
                P.op("pool", lambda e: e.memset(kz[0][64:128, :], 0.0), writes=[B_kz[0]])
                P.op("pool", lambda e: e.memset(kz[1][0:64, :], 0.0), writes=[B_kz[1]])
                it = 0
                LA = 3
                PEND_DELAY = 6
                def c_load(h_):
                    hp_, par_ = h_ // 2, h_ % 2
                    if par_ == 0:
                        P.dma("sp", [(qp[hp_ % 2][:, :], fq_d[hp_, :, :])], B_qp[hp_ % 2], writes=[B_qp[hp_ % 2]])
                    P.dma("sp", [(kz[par_][par_ * 64:par_ * 64 + 64, :], fk_d[hp_, par_ * 64:par_ * 64 + 64, :])], B_kz[par_], writes=[B_kz[par_]])
                    P.dma("sp", [(vh[par_][:], fv_d[h_, :, :, :])], B_vh[par_], writes=[B_vh[par_]])

                c_load(0)
                for hp in range(4):
                    qs = hp % 2
                    for par in range(2):
                        h = hp * 2 + par
                        r0 = par * 64
                        if h + 1 < 8:
                            c_load(h + 1)
                        pairs = []
                        for g in g_list:
                            bs = it % 2
                            it += 1
                            for kb in range(4 * g + 4):
                                pairs.append((g, kb, bs))
                        live = {}
                        pend = []

                        def stage1(idx):
                            g, kb, bs = pairs[idx]
                            nk = 4 * g + 4
                            if kb == 0:
                                P.op("pool", lambda e: e.tensor_scalar(
                                    out=biasg[bs][:, 0:nk], in0=negc[:, 0:nk, h], scalar1=negc0[:, 4 * g, h:h + 1], scalar2=None, op0=ALU.subtract),
                                    reads=[B_negc], writes=[B_bg[bs]])
                            jd = kb - 4 * g
                            c0 = max(jd, 0) * 128
                            n = 512 - c0
                            sp_, B_sp = srot2.next()
                            P.op("pe", lambda e: e.matmul(
                                sp_[:, 0:n], lhsT=kz[par][:, kb * 128:(kb + 1) * 128], rhs=qp[qs][:, g * 512 + c0:(g + 1) * 512],
                                start=True, stop=True),
                                reads=[B_kz[par], B_qp[qs]], writes=[B_sp])
                            pt_, B_pt_ = ptrot.next()
                            P.op("act", lambda e: e.activation(
                                out=pt_[:, 0:n], in_=sp_[:, 0:n], func=AF.Exp, bias=biasg[bs][:, kb:kb + 1], scale=0.125),
                                reads=[B_sp, B_bg[bs]], writes=[B_pt_])
                            if jd >= 0:
                                P.op("pool", lambda e: e.tensor_tensor(out=pt_[:, 0:128], in0=pt_[:, 0:128], in1=tri_b, op=ALU.mult),
                                     reads=[B_pt_, B_const], writes=[B_pt_])
                            live[idx] = (pt_, B_pt_, c0, n)

                        def finish_a(g, bs):
                            ya, B_ya = yac[bs], B_yac[bs]
                            P.op("dve", lambda e: e.tensor_scalar(out=rr[bs][64:65, :], in0=ya[64:65, :], scalar1=1e-30, scalar2=None, op0=ALU.max),
                                 reads=[B_ya], writes=[B_rr[bs]])
                            P.op("dve", lambda e: e.reciprocal(out=rr[bs][64:65, :], in_=rr[bs][64:65, :]), reads=[B_rr[bs]], writes=[B_rr[bs]])

                        def finish_b(g, bs):
                            ya, B_ya = yac[bs], B_yac[bs]
                            P.op("pe", lambda e: e.matmul(rb[0:64, :], lhsT=sel64_f, rhs=rr[bs][:, :], start=True, stop=True),
                                 reads=[B_rr[bs], B_const], writes=[B_rb])
                            P.op("act", lambda e: e.copy(out=rbs[:], in_=rb[0:64, :]), reads=[B_rb], writes=[B_rbs])
                            P.op("dve", lambda e: e.tensor_tensor(out=yo[bs][:], in0=ya[0:64, :], in1=rbs[:], op=ALU.mult),
                                 reads=[B_ya, B_rbs], writes=[B_yo[bs]])
                            P.dma("sp", [(yfx_d[hp, r0:r0 + 64, g * 512:(g + 1) * 512], yo[bs][:])], B_yo[bs], reads=[B_yo[bs]])

                        def stage2(idx):
                            g, kb, bs = pairs[idx]
                            nk = 4 * g + 4
                            pt_, B_pt_, c0, n = live.pop(idx)
                            ya, B_ya = yac[bs], B_yac[bs]
                            P.op("pe", lambda e: e.matmul(
                                ya[0:65, c0:512], lhsT=vh[par][:, kb, :], rhs=pt_[:, 0:n], start=(kb == 0), stop=(kb == nk - 1)),
                                reads=[B_vh[par], B_pt_], writes=[B_ya], sig=(kb == nk - 1))
                            if kb == nk - 1:
                                finish_a(g, bs)
                                pend.append([PEND_DELAY, g, bs])

                        for i in range(len(pairs) + LA):
                            if i < len(pairs):
                                stage1(i)
                            if i - LA >= 0:
                                stage2(i - LA)
                            for pf in list(pend):
                                pf[0] -= 1
                                if pf[0] < 0:
                                    finish_b(pf[1], pf[2])
                                    pend.remove(pf)
                        for pf in pend:
                            finish_b(pf[1], pf[2])
                P.run_phase("C%d" % l)
            if stop_after == ("C", l):
                break

            with ExitStack() as st:
                swab = sb("D_swab%d" % l, [128, 8, 256], F32, st)
                B_swab = Buf()
                P.dma("sp", [(swab[:], swab_in[:, :, :])], B_swab, writes=[B_swab])
                P.op("act", lambda e: e.activation(out=swab[:], in_=swab[:], func=AF.Exp), reads=[B_swab], writes=[B_swab])
                kz = [sb("D_kz%d_%d" % (l, i), [128, T], BF16, st) for i in range(2)]
                B_kz = [Buf() for _ in range(2)]
                qp = [sb("D_qp%d_%d" % (l, i), [128, T], BF16, st) for i in range(2)]
                B_qp = [Buf() for _ in range(2)]
                vh = [sb("D_vh%d_%d" % (l, i), [128, NB, 65], BF16, st) for i in range(2)]
                B_vh = [Buf() for _ in range(2)]
                tmpb = [sb("D_tmp%d_%d" % (l, i), [128, 256], F32, st) for i in range(3)]
                trot = Rot([(tmpb[i], Buf()) for i in range(3)])
                pts = [sb("D_pt%d_%d" % (l, i), [128, 256], BF16, st) for i in range(6)]
                ptrot = Rot([(pts[i], Buf()) for i in range(6)])
                sps = [ps("D_s%d_%d" % (l, i), [128, 512], F32, st) for i in range(5)]
                srot2 = Rot([(sps[i], Buf()) for i in range(5)])
                yac = [ps("D_y%d_%d" % (l, i), [128, 512], F32, st) for i in range(2)]
                B_yac = [Buf() for _ in range(2)]
                rb = ps("D_rb%d" % l, [128, 512], F32, st)
                B_rb = Buf()
                rr = [sb("D_rr%d_%d" % (l, i), [128, 512], F32, st) for i in range(2)]
                B_rr = [Buf() for _ in range(2)]
                rbs = sb("D_rbs%d" % l, [64, 512], F32, st)
                B_rbs = Buf()
                yo = [sb("D_yo%d_%d" % (l, i), [64, 512], BF16, st) for i in range(2)]
                B_yo = [Buf() for _ in range(2)]
                for i_ in range(2):
                    P.op("pool", lambda e: e.memset(rr[i_][:], 0.0), writes=[B_rr[i_]])
                P.op("pool", lambda e: e.memset(kz[0][64:128, :], 0.0), writes=[B_kz[0]])
                P.op("pool", lambda e: e.memset(kz[1][0:64, :], 0.0), writes=[B_kz[1]])
                it = 0
                LA = 3
                PEND_DELAY = 3
                def d_load(h_):
                    hp_, par_ = h_ // 2, h_ % 2
                    kvh_ = h_ // 4
                    if par_ == 0:
                        P.dma("sp", [(qp[hp_ % 2][:, :], sq_d[hp_, :, :])], B_qp[hp_ % 2], writes=[B_qp[hp_ % 2]])
                    P.dma("sp", [(kz[par_][par_ * 64:par_ * 64 + 64, :], sk_d[kvh_ * 64:kvh_ * 64 + 64, :])], B_kz[par_], writes=[B_kz[par_]])
                    P.dma("sp", [(vh[par_][:], sv_d[kvh_, :, :, :])], B_vh[par_], writes=[B_vh[par_]])

                d_load(0)
                for hp in range(4):
                    qs = hp % 2
                    for par in range(2):
                        h = hp * 2 + par
                        kvh = h // 4
                        r0 = par * 64
                        if h + 1 < 8:
                            d_load(h + 1)
                        pairs = []
                        for g in g_list:
                            bs = it % 2
                            it += 1
                            kbs = [kb for kb in range(4 * g - 1, 4 * g + 4) if kb >= 0]
                            for ki, kb in enumerate(kbs):
                                pairs.append((g, kb, bs, ki, len(kbs)))
                        live = {}
                        pend = []

                        def stage1(idx):
                            g, kb, bs, ki, nkk = pairs[idx]
                            qlo = max(kb, 4 * g)
                            qhi = min(kb + 1, 4 * g + 3)
                            c0 = (qlo - 4 * g) * 128
                            n = (qhi - qlo + 1) * 128
                            tb0 = 0 if qlo == kb else 128
                            sp_, B_sp = srot2.next()
                            P.op("pe", lambda e: e.matmul(
                                sp_[:, 0:n], lhsT=kz[par][:, kb * 128:(kb + 1) * 128], rhs=qp[qs][:, g * 512 + c0:g * 512 + c0 + n],
                                start=True, stop=True),
                                reads=[B_kz[par], B_qp[qs]], writes=[B_sp])
                            pt_, B_pt_ = ptrot.next()
                            P.op("act", lambda e: e.activation(out=pt_[:, 0:n], in_=sp_[:, 0:n], func=AF.Exp, scale=0.125),
                                 reads=[B_sp], writes=[B_pt_])
                            P.op(("pool" if idx % 2 == 0 else "dve"), lambda e: e.tensor_tensor(out=pt_[:, 0:n], in0=pt_[:, 0:n], in1=swab[:, h, tb0:tb0 + n], op=ALU.mult),
                                 reads=[B_pt_, B_swab], writes=[B_pt_])
                            live[idx] = (pt_, B_pt_, c0, n)

                        def finish_a(g, bs):
                            ya, B_ya = yac[bs], B_yac[bs]
                            P.op("act", lambda e: e.activation(out=rr[bs][64:65, :], in_=ya[64:65, :], func=AF.Ln, bias=esink[64:65, l, h:h + 1]),
                                 reads=[B_ya, B_const], writes=[B_rr[bs]])
                            P.op("act", lambda e: e.activation(out=rr[bs][64:65, :], in_=rr[bs][64:65, :], func=AF.Exp, scale=-1.0),
                                 reads=[B_rr[bs]], writes=[B_rr[bs]])

                        def finish_b(g, bs):
                            ya, B_ya = yac[bs], B_yac[bs]
                            P.op("pe", lambda e: e.matmul(rb[0:64, :], lhsT=sel64_f, rhs=rr[bs][:, :], start=True, stop=True),
                                 reads=[B_rr[bs], B_const], writes=[B_rb])
                            P.op("act", lambda e: e.copy(out=rbs[:], in_=rb[0:64, :]), reads=[B_rb], writes=[B_rbs])
                            P.op("dve", lambda e: e.tensor_tensor(out=yo[bs][:], in0=ya[0:64, :], in1=rbs[:], op=ALU.mult),
                                 reads=[B_ya, B_rbs], writes=[B_yo[bs]])
                            P.dma("sp", [(ysw_d[hp, r0:r0 + 64, g * 512:(g + 1) * 512], yo[bs][:])], B_yo[bs], reads=[B_yo[bs]])

                        def stage2(idx):
                            g, kb, bs, ki, nkk = pairs[idx]
                            pt_, B_pt_, c0, n = live.pop(idx)
                            ya, B_ya = yac[bs], B_yac[bs]
                            P.op("pe", lambda e: e.matmul(
                                ya[0:65, c0:c0 + n], lhsT=vh[par][:, kb, :], rhs=pt_[:, 0:n], start=(ki == 0), stop=(ki == nkk - 1), skip_group_check=True),
                                reads=[B_vh[par], B_pt_], writes=[B_ya], sig=(ki == nkk - 1))
                            if ki == nkk - 1:
                                finish_a(g, bs)
                                pend.append([PEND_DELAY, g, bs])

                        for i in range(len(pairs) + LA):
                            if i < len(pairs):
                                stage1(i)
                            if i - LA >= 0:
                                stage2(i - LA)
                            for pf in list(pend):
                                pf[0] -= 1
                                if pf[0] < 0:
                                    finish_b(pf[1], pf[2])
                                    pend.remove(pf)
                        for pf in pend:
                            finish_b(pf[1], pf[2])
                P.run_phase("D%d" % l)
            if stop_after == ("D", l):
                break

            with ExitStack() as st:
                wg = sb("E_wg%d" % l, [128, 8, 3072], BF16, st)
                wbr = sb("E_wbr%d" % l, [128, 12, D], BF16, st)
                wo = sb("E_wo%d" % l, [128, 8, D], BF16, st)
                B_wgq = [Buf() for _ in range(4)]
                B_wbrq = [Buf() for _ in range(4)]
                B_wo = Buf()
                wbr_src = w_branch[l].rearrange("b r n -> (b r) n")
                for q in range(4):
                    P.dma("pool", [(wg[:, k, b * 1024 + q * 256:b * 1024 + (q + 1) * 256],
                                    w_in[l, k * 128:(k + 1) * 128, C_G + b * 1024 + q * 256:C_G + b * 1024 + (q + 1) * 256]) for b in range(3) for k in range(8)],
                          B_wgq[q], writes=[B_wgq[q]])
                    P.dma("pool", [(wbr[:, k, q * 256:(q + 1) * 256], wbr_src[k * 128:(k + 1) * 128, q * 256:(q + 1) * 256]) for k in range(12)],
                          B_wbrq[q], writes=[B_wbrq[q]])
                P.dma("pool", [(wo[:, k, :], w_mix_out[l, k * 128:(k + 1) * 128, :]) for k in range(8)], B_wo, writes=[B_wo])
                hT = [sb("E_hT%d_%d" % (l, i), [128, 8, 512], BF16, st) for i in range(2)]
                B_hT = [Buf() for _ in range(2)]
                yb = [sb("E_yb%d_%d" % (l, i), [128, 12, 512], BF16, st) for i in range(2)]
                B_yb = [Buf() for _ in range(2)]
                mT = sb("E_mT%d" % l, [128, 8, 512], BF16, st)
                B_mT = Buf()
                sgs = [sb("E_sg%d_%d" % (l, i), [128, 512], F32, st) for i in range(3)]
                sgrot = Rot([(sgs[i], Buf()) for i in range(3)])
                tbs = [sb("E_tb%d_%d" % (l, i), [128, 512], F32, st) for i in range(4)]
                tbrot = Rot([(tbs[i], Buf()) for i in range(4)])
                m01 = sb("E_m01%d" % l, [128, 512], F32, st)
                B_m01 = Buf()
                xb = [sb("E_x%d_%d" % (l, i), [128, D], F32, st) for i in range(2)]
                B_xb = [Buf() for _ in range(2)]
                pbanks = [ps("E_ps%d_%d" % (l, i), [128, 512], F32, st) for i in range(8)]
                prot = Rot([(pbanks[i], Buf()) for i in range(8)])
                ysrc = [ycv_d, yfx_d, ysw_d]
                def e_load(ti):
                    t = full_tiles[ti]
                    s = ti % 2
                    tsl = slice(t * 512, (t + 1) * 512)
                    P.dma("sp", [(hT[s][:], hT_d[:, :, tsl].rearrange("k p t -> p k t"))], B_hT[s], writes=[B_hT[s]])
                    P.dma("sp", [(yb[s][:, b * 4:(b + 1) * 4, :], ysrc[b][:, :, tsl].rearrange("c p t -> p c t")) for b in range(3)],
                          B_yb[s], writes=[B_yb[s]])

                e_load(0)
                for ti, t in enumerate(full_tiles):
                    s = ti % 2
                    if ti + 1 < len(full_tiles):
                        e_load(ti + 1)
                    for fc in ([] if "E_nofc" in DBG else range(8)):
                        tbl = []
                        for b in range(3):
                            pg, B_pg = prot.next()
                            for kc in range(8):
                                P.op("pe", lambda e, pg=pg, kc=kc, b=b, fc=fc, s=s: e.matmul(
                                    pg[:], lhsT=wg[:, kc, b * 1024 + fc * 128:b * 1024 + (fc + 1) * 128], rhs=hT[s][:, kc, :], start=(kc == 0), stop=(kc == 7)),
                                    reads=[B_wgq[fc // 2], B_hT[s]], writes=[B_pg], sig=(kc == 7))
                            pp, B_pp = prot.next()
                            for kc in range(4):
                                P.op("pe", lambda e, pp=pp, kc=kc, b=b, fc=fc, s=s: e.matmul(
                                    pp[:], lhsT=wbr[:, b * 4 + kc, fc * 128:(fc + 1) * 128], rhs=yb[s][:, b * 4 + kc, :], start=(kc == 0), stop=(kc == 3)),
                                    reads=[B_wbrq[fc // 2], B_yb[s]], writes=[B_pp], sig=(kc == 3))
                            sg, B_sg = sgrot.next()
                            P.op("act", lambda e, sg=sg, pg=pg: e.activation(out=sg[:], in_=pg[:], func=(AF.Exp if "E_nosig" in DBG else AF.Sigmoid)), reads=[B_pg], writes=[B_sg])
                            tb, B_tb = tbrot.next()
                            P.op("dve", lambda e, tb=tb, sg=sg, pp=pp: e.tensor_tensor(out=tb[:], in0=pp[:], in1=sg[:], op=ALU.mult),
                                 reads=[B_pp, B_sg], writes=[B_tb])
                            tbl.append((tb, B_tb))
                        aeng = "dve" if "E_nopool" in DBG else "pool"
                        P.op(aeng, lambda e, a=tbl[0][0], b_=tbl[1][0]: e.tensor_tensor(out=m01[:], in0=a[:], in1=b_[:], op=ALU.add),
                             reads=[tbl[0][1], tbl[1][1]], writes=[B_m01])
                        P.op(aeng, lambda e, c_=tbl[2][0], fc=fc: e.tensor_tensor(out=mT[:, fc, :], in0=m01[:], in1=c_[:], op=ALU.add),
                             reads=[B_m01, tbl[2][1]], writes=[B_mT])
                    for j in ([] if "E_noout" in DBG else range(4)):
                        blk = t * 4 + j
                        xs = blk % 2
                        P.dma("sp", [(xb[xs][:], x_src[blk * 128:(blk + 1) * 128, :])], B_xb[xs], writes=[B_xb[xs]])
                        for half in range(2):
                            po, B_po = prot.next()
                            for kc in range(8):
                                P.op("pe", lambda e, po=po, kc=kc, j=j, half=half: e.matmul(
                                    po[:], lhsT=mT[:, kc, j * 128:(j + 1) * 128], rhs=wo[:, kc, half * 512:(half + 1) * 512], start=(kc == 0), stop=(kc == 7)),
                                    reads=[B_mT, B_wo], writes=[B_po], sig=(kc == 7))
                            P.op("dve", lambda e, po=po, xs=xs, half=half: e.tensor_tensor(
                                out=xb[xs][:, half * 512:(half + 1) * 512], in0=po[:], in1=xb[xs][:, half * 512:(half + 1) * 512], op=ALU.add),
                                reads=[B_po, B_xb[xs]], writes=[B_xb[xs]])
                        P.dma("pool", [(xres[blk * 128:(blk + 1) * 128, :], xb[xs][:])], B_xb[xs], reads=[B_xb[xs]])
                P.run_phase("E%d" % l)
            if stop_after == ("E", l):
                break

            with ExitStack() as st:
                wq = sb("F_wq%d" % l, [128, 8, D], BF16, st)
                wkv = sb("F_wkv%d" % l, [128, 8, 2 * D], BF16, st)
                wo = sb("F_wo%d" % l, [128, 8, D], BF16, st)
                B_wq, B_wkv, B_wo = Buf(), Buf(), Buf()
                B_wkvq = [Buf() for _ in range(4)]
                B_wqq = [Buf() for _ in range(2)]
                for q in range(4):
                    P.dma("pool", [(wkv[:, k, q * 512:(q + 1) * 512], w_xkv[l, k * 128:(k + 1) * 128, q * 512:(q + 1) * 512]) for k in range(8)],
                          B_wkvq[q], writes=[B_wkvq[q]])
                for q in range(2):
                    P.dma("pool", [(wq[:, k, q * 512:(q + 1) * 512], w_xq[l, k * 128:(k + 1) * 128, q * 512:(q + 1) * 512]) for k in range(8)],
                          B_wqq[q], writes=[B_wqq[q]])
                P.dma("pool", [(wo[:, k, :], w_xo[l, k * 128:(k + 1) * 128, :]) for k in range(8)], B_wo, writes=[B_wo])
                gM = gain_tile(st, 4 + l)
                gX = gain_tile(st, 2 + l)
                xt = [sb("F_xt%d_%d" % (l, i), [128, 4, D], F32, st) for i in range(2)]
                B_xt = [Buf() for _ in range(2)]
                junk = sb("F_junk%d" % l, [128, D], BF16, st)
                ssb = [sb("F_ss%d_%d" % (l, i), [128, 4], F32, st) for i in range(2)]
                B_tmp = [Buf() for _ in range(2)]
                hb = [sb("F_h%d_%d" % (l, i), [128, D], BF16, st) for i in range(4)]
                B_hb = [Buf() for _ in range(4)]
                hT2 = [sb("F_hT%d_%d" % (l, i), [128, 8, 512], BF16, st) for i in range(2)]
                B_hT2 = [Buf() for _ in range(2)]
                memT = sb("F_memT%d" % l, [128, 8, 256], BF16, st)
                B_memT = Buf()
                kxT = sb("F_kxT%d" % l, [128, 8, 256], BF16, st)
                vx = sb("F_vx%d" % l, [128, 2, D], BF16, st)
                B_kx, B_vx = Buf(), Buf()
                qxT = sb("F_qxT%d" % l, [128, 8, 512], BF16, st)
                B_qx = Buf()
                pTs = [sb("F_pT%d_%d" % (l, i), [128, 512], BF16, st) for i in range(4)]
                B_pT = [Buf() for _ in range(4)]
                yxT = sb("F_yxT%d" % l, [128, 8, 512], BF16, st)
                B_yx = Buf()
                rr = sb("F_rr%d" % l, [128, 512], F32, st)
                B_rr = Buf()
                rbs2 = [sb("F_rbs%d_%d" % (l, i), [128, 512], F32, st) for i in range(2)]
                B_rbs2 = [Buf() for _ in range(2)]
                ptr = ps("F_pt%d" % l, [128, 8, 128], BF16, st)
                B_ptr = Buf()
                pbanks = [ps("F_ps%d_%d" % (l, i), [128, 512], F32, st) for i in range(3)]
                prot = Rot([(pbanks[i], Buf()) for i in range(3)])
                yps = [ps("F_y%d_%d" % (l, i), [128, 512], F32, st) for i in range(4)]
                B_yps = [Buf() for _ in range(4)]
                for mb in range(2):
                    P.dma("sp", [(xt[0][:, mb, :], mem_in[mb * 128:(mb + 1) * 128, :])], B_xt[0], writes=[B_xt[0]])
                    rmsnorm_block(P, None, xt[0][:, mb, :], gM, hb[mb][:], B_xt[0], B_hb[mb], (junk, ssb[mb], B_tmp[mb]))
                    transpose_block(P, hb[mb], B_hb[mb], ptr, B_ptr, memT[:, :, mb * 128:(mb + 1) * 128], B_memT, eng="act")
                for c in range(8):
                    pb, B_pb = prot.next()
                    for kc in range(8):
                        P.op("pe", lambda e, pb=pb, kc=kc, c=c: e.matmul(pb[:, 0:256], lhsT=wkv[:, kc, c * 128:(c + 1) * 128], rhs=memT[:, kc, :],
                                                                        start=(kc == 0), stop=(kc == 7)),
                             reads=[B_wkvq[c // 4], B_memT], writes=[B_pb], sig=(kc == 7))
                    P.op("dve", lambda e, pb=pb, c=c: e.tensor_copy(out=kxT[:, c, :], in_=pb[:, 0:256]), reads=[B_pb], writes=[B_kx])
                for mb in range(2):
                    for half in range(2):
                        pb, B_pb = prot.next()
                        for kc in range(8):
                            P.op("pe", lambda e, pb=pb, kc=kc, mb=mb, half=half: e.matmul(
                                pb[:], lhsT=memT[:, kc, mb * 128:(mb + 1) * 128], rhs=wkv[:, kc, D + half * 512:D + (half + 1) * 512],
                                start=(kc == 0), stop=(kc == 7)),
                                reads=[B_wkvq[2 + half], B_memT], writes=[B_pb], sig=(kc == 7))
                        P.op("act", lambda e, pb=pb, mb=mb, half=half: e.copy(out=vx[:, mb, half * 512:(half + 1) * 512], in_=pb[:]),
                             reads=[B_pb], writes=[B_vx])
                def f_load(ti):
                    t = full_tiles[ti]
                    s = ti % 2
                    P.dma("sp", [(xt[s][:, j, :], xres[(t * 4 + j) * 128:(t * 4 + j + 1) * 128, :]) for j in range(4)], B_xt[s], writes=[B_xt[s]])

                def f_norm_ew(ti, blocks=range(4)):
                    s = ti % 2
                    for j in blocks:
                        rmsnorm_block(P, None, xt[s][:, j, :], gX, hb[j][:], B_xt[s], B_hb[j], (junk, ssb[j % 2], B_tmp[j % 2]))

                def f_norm_tr(ti):
                    s = ti % 2
                    for j in range(4):
                        transpose_block(P, hb[j], B_hb[j], ptr, B_ptr, hT2[s][:, :, j * 128:(j + 1) * 128], B_hT2[s],
                                        eng=("act" if j % 2 == 0 else "dve"))

                def f_main(ti):
                    t = full_tiles[ti]
                    s = ti % 2
                    hT, B_hT = hT2[s], B_hT2[s]
                    if ti + 1 < len(full_tiles):
                        f_load(ti + 1)
                    for c in range(8):
                        pb, B_pb = prot.next()
                        for kc in range(8):
                            P.op("pe", lambda e, pb=pb, kc=kc, c=c: e.matmul(pb[:], lhsT=wq[:, kc, c * 128:(c + 1) * 128], rhs=hT[:, kc, :],
                                                                            start=(kc == 0), stop=(kc == 7)),
                                 reads=[B_wqq[c // 4], B_hT], writes=[B_pb], sig=(kc == 7))
                        if c % 2 == 0:
                            P.op("act", lambda e, pb=pb, c=c: e.copy(out=qxT[:, c, :], in_=pb[:]), reads=[B_pb], writes=[B_qx])
                        else:
                            P.op("dve", lambda e, pb=pb, c=c: e.tensor_copy(out=qxT[:, c, :], in_=pb[:]), reads=[B_pb], writes=[B_qx])
                    def f_s(hd):
                        for mb in range(2):
                            pb, B_pb = prot.next()
                            for dc in range(2):
                                P.op("pe", lambda e: e.matmul(
                                    pb[:], lhsT=kxT[:, hd * 2 + dc, mb * 128:(mb + 1) * 128], rhs=qxT[:, hd * 2 + dc, :], start=(dc == 0), stop=(dc == 1)),
                                    reads=[B_kx, B_qx], writes=[B_pb], sig=(dc == 1))
                            pi = (hd % 2) * 2 + mb
                            P.op("act", lambda e: e.activation(out=pTs[pi][:], in_=pb[:], func=AF.Exp, scale=1.0 / 16.0),
                                 reads=[B_pb], writes=[B_pT[pi]])

                    def f_pv(hd):
                        par = hd % 2
                        dpb, B_dpb = prot.next()
                        for mb in range(2):
                            pi = par * 2 + mb
                            P.op("pe", lambda e: e.matmul(dpb[:], lhsT=ones_f_b, rhs=pTs[pi][:], start=(mb == 0), stop=(mb == 1)),
                                 reads=[B_pT[pi], B_const], writes=[B_dpb], sig=(mb == 1))
                        for dc in range(2):
                            yp, B_yp = yps[par * 2 + dc], B_yps[par * 2 + dc]
                            for mb in range(2):
                                pi = par * 2 + mb
                                P.op("pe", lambda e: e.matmul(
                                    yp[:], lhsT=vx[:, mb, hd * 256 + dc * 128:hd * 256 + (dc + 1) * 128], rhs=pTs[pi][:], start=(mb == 0), stop=(mb == 1)),
                                    reads=[B_pT[pi], B_vx], writes=[B_yp], sig=(mb == 1))
                        rb_, B_rb_ = rbs2[par], B_rbs2[par]
                        P.op("act", lambda e: e.activation(out=rb_[:], in_=dpb[:], func=AF.Ln), reads=[B_dpb], writes=[B_rb_])
                        P.op("act", lambda e: e.activation(out=rb_[:], in_=rb_[:], func=AF.Exp, scale=-1.0), reads=[B_rb_], writes=[B_rb_])
                        for dc in range(2):
                            yp, B_yp = yps[par * 2 + dc], B_yps[par * 2 + dc]
                            P.op("dve", lambda e: e.tensor_tensor(out=yxT[:, hd * 2 + dc, :], in0=yp[:], in1=rb_[:], op=ALU.mult),
                                 reads=[B_yp, B_rb_], writes=[B_yx])

                    f_s(0)
                    for hd in range(4):
                        if hd + 1 < 4:
                            f_s(hd + 1)
                        f_pv(hd)

                def f_main_b(ti):
                    t = full_tiles[ti]
                    s = ti % 2
                    for j in range(4):
                        blk = t * 4 + j
                        for half in range(2):
                            po, B_po = prot.next()
                            for kc in range(8):
                                P.op("pe", lambda e, po=po, kc=kc, j=j, half=half: e.matmul(
                                    po[:], lhsT=yxT[:, kc, j * 128:(j + 1) * 128], rhs=wo[:, kc, half * 512:(half + 1) * 512], start=(kc == 0), stop=(kc == 7)),
                                    reads=[B_yx, B_wo], writes=[B_po], sig=(kc == 7))
                            P.op("dve", lambda e, po=po, s=s, j=j, half=half: e.tensor_tensor(
                                out=xt[s][:, j, half * 512:(half + 1) * 512], in0=po[:], in1=xt[s][:, j, half * 512:(half + 1) * 512], op=ALU.add),
                                reads=[B_po, B_xt[s]], writes=[B_xt[s]])
                        if ti + 1 < len(full_tiles):
                            f_norm_ew(ti + 1, blocks=[j])
                    P.dma("pool", [(xres[t * 512:(t + 1) * 512, :].rearrange("(j p) d -> p j d", p=128), xt[s][:])], B_xt[s], reads=[B_xt[s]])

                f_load(0)
                f_norm_ew(0)
                f_norm_tr(0)
                for ti in range(len(full_tiles)):
                    f_main(ti)
                    f_main_b(ti)
                    if ti + 1 < len(full_tiles):
                        f_norm_tr(ti + 1)
                P.run_phase("F%d" % l)
            if stop_after == ("F", l):
                break

            with ExitStack() as st:
                wg = sb("G_wg%d" % l, [128, 8, DFF], BF16, st)
                wu = sb("G_wu%d" % l, [128, 8, DFF], BF16, st)
                gcols = [(0, 768), (768, 1536), (1536, 2176), (2176, DFF)]

                def gq(c):
                    return 0 if c < 6 else (1 if c < 12 else (2 if c < 17 else 3))
                B_wgq = [Buf() for _ in range(4)]
                B_wuq = [Buf() for _ in range(4)]
                for q, (c0_, c1_) in enumerate(gcols):
                    P.dma("pool", [(wg[:, k, c0_:c1_], w_fg[l, k * 128:(k + 1) * 128, c0_:c1_]) for k in range(8)], B_wgq[q], writes=[B_wgq[q]])
                    P.dma("pool", [(wu[:, k, c0_:c1_], w_fu[l, k * 128:(k + 1) * 128, c0_:c1_]) for k in range(8)], B_wuq[q], writes=[B_wuq[q]])
                gF = gain_tile(st, 6 + l)
                xb = [sb("G_x%d_%d" % (l, i), [128, D], F32, st) for i in range(4)]
                B_xb = [Buf() for _ in range(4)]
                junk = sb("G_junk%d" % l, [128, D], BF16, st)
                ssb = [sb("G_ss%d_%d" % (l, i), [128, 4], F32, st) for i in range(2)]
                B_tmp = [Buf() for _ in range(2)]
                hb = [sb("G_h%d_%d" % (l, i), [128, D], BF16, st) for i in range(4)]
                B_hb = [Buf() for _ in range(4)]
                hT = [sb("G_hT%d_%d" % (l, i), [128, 8, 512], BF16, st) for i in range(2)]
                B_hT = [Buf() for _ in range(2)]
                aT = [sb("G_aT%d_%d" % (l, i), [128, NFF, 512], BF16, st) for i in range(2)]
                B_aT = [Buf() for _ in range(2)]
                sil = [sb("G_sil%d_%d" % (l, i), [128, 512], F32, st) for i in range(3)]
                silrot = Rot([(sil[i], Buf()) for i in range(3)])
                ptr2 = [ps("G_pt%d_%d" % (l, i), [128, 8, 128], BF16, st) for i in range(1)] * 2
                B_ptr2 = [Buf()] * 2
                pbanks = [ps("G_ps%d_%d" % (l, i), [128, 512], F32, st) for i in range(7)]
                prot = Rot([(pbanks[i], Buf()) for i in range(7)])
                def g_norm_ew(ti):
                    t = full_tiles[ti]
                    for j in range(4):
                        blk = t * 4 + j
                        P.dma("sp", [(xb[j][:], xres[blk * 128:(blk + 1) * 128, :])], B_xb[j], writes=[B_xb[j]])
                        rmsnorm_block(P, None, xb[j][:], gF, hb[j][:], B_xb[j], B_hb[j], (junk, ssb[j % 2], B_tmp[j % 2]))

                def g_norm_tr(ti):
                    s = ti % 2
                    for j in range(4):
                        transpose_block(P, hb[j], B_hb[j], ptr2[j % 2], B_ptr2[j % 2], hT[s][:, :, j * 128:(j + 1) * 128], B_hT[s],
                                        eng=("act" if j % 2 == 0 else "dve"))

                def g_main(ti):
                    t = full_tiles[ti]
                    s = ti % 2
                    for c in range(NFF):
                        pg, B_pg = prot.next()
                        for kc in range(8):
                            P.op("pe", lambda e, pg=pg, kc=kc, c=c, s=s: e.matmul(pg[:], lhsT=wg[:, kc, c * 128:(c + 1) * 128], rhs=hT[s][:, kc, :],
                                                                                 start=(kc == 0), stop=(kc == 7)),
                                 reads=[B_wgq[gq(c)], B_hT[s]], writes=[B_pg], sig=(kc == 7))
                        pu, B_pu = prot.next()
                        for kc in range(8):
                            P.op("pe", lambda e, pu=pu, kc=kc, c=c, s=s: e.matmul(pu[:], lhsT=wu[:, kc, c * 128:(c + 1) * 128], rhs=hT[s][:, kc, :],
                                                                                 start=(kc == 0), stop=(kc == 7)),
                                 reads=[B_wuq[gq(c)], B_hT[s]], writes=[B_pu], sig=(kc == 7))
                        sl, B_sl = silrot.next()
                        P.op("act", lambda e, sl=sl, pg=pg: e.activation(out=sl[:], in_=pg[:], func=AF.Silu), reads=[B_pg], writes=[B_sl])
                        P.op("dve", lambda e, sl=sl, pu=pu, c=c, s=s: e.tensor_tensor(out=aT[s][:, c, :], in0=pu[:], in1=sl[:], op=ALU.mult),
                             reads=[B_pu, B_sl], writes=[B_aT[s]])
                        if c == 8 and ti + 1 < len(full_tiles):
                            g_norm_ew(ti + 1)
                    P.dma("sp", [(aT_d[:, :, t * 512:(t + 1) * 512].rearrange("c p t -> p c t"), aT[s][:])], B_aT[s], reads=[B_aT[s]])

                g_norm_ew(0)
                g_norm_tr(0)
                for ti in range(len(full_tiles)):
                    g_main(ti)
                    if ti + 1 < len(full_tiles):
                        g_norm_tr(ti + 1)
                P.run_phase("G1_%d" % l)

            with ExitStack() as st:
                last = (l == n_layers - 1)
                wd = sb("H_wd%d" % l, [128, NFF, D], BF16, st)
                B_wdh = [Buf() for _ in range(2)]
                for hh in range(2):
                    P.dma("pool", [(wd[:, k, hh * 512:(hh + 1) * 512], w_fd[l, k * 128:(k + 1) * 128, hh * 512:(hh + 1) * 512]) for k in range(NFF)],
                          B_wdh[hh], writes=[B_wdh[hh]])
                if last:
                    gL = gain_tile(st, 8)
                    junk = sb("H_junk%d" % l, [128, D], BF16, st)
                    ssb = [sb("H_ss%d_%d" % (l, i), [128, 4], F32, st) for i in range(2)]
                    B_tmp = [Buf() for _ in range(2)]
                    ob = [sb("H_o%d_%d" % (l, i), [128, D], F32, st) for i in range(2)]
                    B_ob = [Buf() for _ in range(2)]
                aT = [sb("H_aT%d_%d" % (l, i), [128, NFF, 512], BF16, st) for i in range(2)]
                B_aT = [Buf() for _ in range(2)]
                xb = [sb("H_x%d_%d" % (l, i), [128, D], F32, st) for i in range(2)]
                B_xb = [Buf() for _ in range(2)]
                pbanks = [ps("H_ps%d_%d" % (l, i), [128, 512], F32, st) for i in range(8)]
                prot = Rot([(pbanks[i], Buf()) for i in range(8)])
                def h_load(ti):
                    t = full_tiles[ti]
                    s = ti % 2
                    P.dma("sp", [(aT[s][:], aT_d[:, :, t * 512:(t + 1) * 512].rearrange("c p t -> p c t"))], B_aT[s], writes=[B_aT[s]])

                h_load(0)
                for ti, t in enumerate(full_tiles):
                    s = ti % 2
                    if ti + 1 < len(full_tiles):
                        h_load(ti + 1)
                    for j in range(4):
                        blk = t * 4 + j
                        xs = blk % 2
                        P.dma("sp", [(xb[xs][:], xres[blk * 128:(blk + 1) * 128, :])], B_xb[xs], writes=[B_xb[xs]])
                        for half in range(2):
                            po, B_po = prot.next()
                            for c in range(NFF):
                                P.op("pe", lambda e, po=po, c=c, j=j, half=half, s=s: e.matmul(
                                    po[:], lhsT=aT[s][:, c, j * 128:(j + 1) * 128], rhs=wd[:, c, half * 512:(half + 1) * 512], start=(c == 0), stop=(c == NFF - 1)),
                                    reads=[B_aT[s], B_wdh[half]], writes=[B_po], sig=(c == NFF - 1))
                            P.op("dve", lambda e, po=po, xs=xs, half=half: e.tensor_tensor(
                                out=xb[xs][:, half * 512:(half + 1) * 512], in0=po[:], in1=xb[xs][:, half * 512:(half + 1) * 512], op=ALU.add),
                                reads=[B_po, B_xb[xs]], writes=[B_xb[xs]])
                        if last:
                            rmsnorm_block(P, None, xb[xs][:], gL, ob[xs][:], B_xb[xs], B_ob[xs], (junk, ssb[xs], B_tmp[xs]))
                            orow = (blk - NPREV_T * 4) * 128
                            P.dma("pool", [(out_d[orow:orow + 128, :], ob[xs][:])], B_ob[xs], reads=[B_ob[xs]])
                        else:
                            P.dma("pool", [(xres[blk * 128:(blk + 1) * 128, :], xb[xs][:])], B_xb[xs], reads=[B_xb[xs]])
                P.run_phase("G2_%d" % l)
            if stop_after == ("G", l):
                break
    return nc


def _t5_bucket_np(n):
    n = np.maximum(n, 0)
    max_exact = 16
    nf = np.maximum(n, 1).astype(np.float32)
    large = max_exact + (np.log(nf / np.float32(max_exact)) / np.float32(np.log(128 / max_exact)) * np.float32(32 - max_exact)).astype(np.int32)
    large = np.minimum(large, 31)
    return np.where(n < max_exact, n, large)


def _consts():
    cf = np.zeros((128, 128 * 3 + 512), np.float32)
    j = np.arange(128)[:, None]
    t = np.arange(128)[None, :]
    cf[:, 0:128] = (j <= t)
    cf[:, 128:256] = 1.0
    cf[0, 256:384] = 1.0
    cf[64, 384:448] = 1.0
    cb = np.zeros((128, 128 * 2 + 16 + 128), np.float32)
    cb[:, 0:128] = np.eye(128)
    cb[:, 128:256] = (j <= t)
    cb[:, 272:400] = 1.0
    return cf, cb.astype(ml_dtypes.bfloat16)


def _swa_bias_table(rel_bias):
    k = np.arange(128)[:, None]
    q = np.arange(128)[None, :]
    tab = np.full((128, 8, 256), NEG, np.float32)
    for part, dist in ((0, q - k), (1, 128 + q - k)):
        ok = (dist >= 0) & (dist < 128)
        bk = _t5_bucket_np(dist)
        for h in range(8):
            vals = rel_bias[bk, h]
            tab[:, h, part * 128:(part + 1) * 128] = np.where(ok, vals, np.float32(NEG))
    return tab


_NC_CACHE = {}


def _get_nc(**kw):
    key = tuple(sorted((k, str(v)) for k, v in kw.items()))
    if key not in _NC_CACHE:
        _NC_CACHE[key] = build_program(**kw)
    return _NC_CACHE[key]


def make_in_maps(inputs, cores=range(8)):
    f32 = lambda a: np.ascontiguousarray(np.asarray(a, dtype=np.float32))
    x = f32(inputs["x"])
    mem = f32(inputs["mem"])
    cf, cb = _consts()
    swab = _swa_bias_table(f32(inputs["rel_bias"]))
    gains = np.concatenate([f32(inputs["mix_norm_g"]), f32(inputs["xattn_norm_g"]), f32(inputs["mem_norm_g"]),
                            f32(inputs["ffn_norm_g"]), f32(inputs["final_norm_g"])[None, :]], axis=0)
    conv_wT = np.ascontiguousarray(np.transpose(f32(inputs["conv_w"]), (0, 2, 1)))
    shared = {
        "cf": cf, "cb": cb, "swab": swab, "gains": gains, "conv_wT": conv_wT,
        "w_in": f32(inputs["w_in"]), "w_branch": f32(inputs["w_branch"]), "w_mix_out": f32(inputs["w_mix_out"]),
        "w_xq": f32(inputs["w_xq"]), "w_xkv": f32(inputs["w_xkv"]), "w_xo": f32(inputs["w_xo"]),
        "w_ffn_gate": f32(inputs["w_ffn_gate"]), "w_ffn_up": f32(inputs["w_ffn_up"]), "w_ffn_down": f32(inputs["w_ffn_down"]),
        "forget_bias": f32(inputs["forget_bias"]), "sink": f32(inputs["sink"]),
    }
    maps = []
    for c in cores:
        b, half = c // 2, c % 2
        if half == 0:
            xc = np.concatenate([np.zeros((T // 2, D), np.float32), x[b, :T // 2]], axis=0)
            valid = np.zeros((128, NB), np.float32)
            valid[:, NB // 2:] = 1.0
        else:
            xc = x[b]
            valid = np.ones((128, NB), np.float32)
        m = dict(shared)
        m["x"] = np.ascontiguousarray(xc)
        m["mem"] = np.ascontiguousarray(mem[b])
        m["valid"] = valid
        maps.append(m)
    return maps


def kernel(**inputs):
    nc = _get_nc()
    maps = make_in_maps(inputs)
    res = run_bass_kernel_spmd(nc, maps, core_ids=list(range(8)))
    out = np.zeros((4, T, D), np.float32)
    for c in range(8):
        b, half = c // 2, c % 2
        out[b, half * (T // 2):(half + 1) * (T // 2)] = res.results[c]["out"]
    return out
```
